# Optimizing a Trainium2 kernel written in Bass

```python
import math
import jax, jax.numpy as jnp
from jax import lax
import numpy as np

D_MODEL = 1024
BATCH = 32
SEQ = 256
DEPTH = 2
DEC_BATCH = 4
DEC_SEQ = 4096
PAST_LEN = 512

GRID_W = 64
SC_WIDTH = D_MODEL // 2
SC_KERNEL = 3
N_HEADS = 8
QK_DIM = 64
V_DIM = 2 * QK_DIM
ATTN_WIDTH = N_HEADS * V_DIM
CF_WIDTH = D_MODEL // 2
CF_KERNEL = 31
D_FF = 4 * D_MODEL
N_BRANCH = 3
Q_BLOCK = 128
ROPE_BASE = 10000.0
EPS = 1e-6
IN_SIZES = (SC_WIDTH, SC_WIDTH, SC_WIDTH,
            N_HEADS * 2 * QK_DIM, N_HEADS * 2 * QK_DIM, ATTN_WIDTH,
            2 * CF_WIDTH, D_MODEL, D_MODEL, D_MODEL)
IN_COLS = 3 * SC_WIDTH + 2 * N_HEADS * 2 * QK_DIM + ATTN_WIDTH + 2 * CF_WIDTH + N_BRANCH * D_MODEL

kernel_name = "hybrid_diff_flow_gated_branches_step"


def rmsnorm(x, g):
    xf = x.astype(jnp.float32)
    y = xf * lax.rsqrt(jnp.mean(xf * xf, axis=-1, keepdims=True) + EPS)
    return y.astype(x.dtype) * g


def layernorm(x, g, b):
    xf = x.astype(jnp.float32)
    mu = jnp.mean(xf, axis=-1, keepdims=True)
    var = jnp.mean(jnp.square(xf - mu), axis=-1, keepdims=True)
    y = (xf - mu) * lax.rsqrt(var + EPS)
    return y.astype(x.dtype) * g + b


def dwconv(x, w):
    k, ch = w.shape
    return lax.conv_general_dilated(x, w[:, None, :], window_strides=(1,),
                                    padding=[(k // 2, k // 2)],
                                    dimension_numbers=("NWC", "WIO", "NWC"),
                                    feature_group_count=ch)


def split_in(u):
    offs, acc = [], 0
    for s in IN_SIZES[:-1]:
        acc += s
        offs.append(acc)
    return jnp.split(u, offs, axis=-1)


def axial_rope(x, row, col):
    half = QK_DIM // 2
    quarter = half // 2
    inv = ROPE_BASE ** (-jnp.arange(quarter, dtype=jnp.float32) / quarter)

    def rot(xa, pos):
        ang = pos.astype(jnp.float32)[:, None] * inv[None, :]
        cos = jnp.cos(ang)[None, :, None, None, :].astype(x.dtype)
        sin = jnp.sin(ang)[None, :, None, None, :].astype(x.dtype)
        x1, x2 = xa[..., :quarter], xa[..., quarter:]
        return jnp.concatenate([x1 * cos - x2 * sin, x2 * cos + x1 * sin], axis=-1)

    return jnp.concatenate([rot(x[..., :half], row), rot(x[..., half:], col)], axis=-1)


def diff_attention(q, k, v, lam, lam_init, subln_g):
    b, tq = q.shape[0], q.shape[1]
    nb = tq // Q_BLOCK
    qb = q.reshape(b, nb, Q_BLOCK, N_HEADS, 2, QK_DIM).swapaxes(0, 1)
    scale = QK_DIM ** -0.5

    def block(qq):
        s = jnp.einsum("bqhcd,bkhcd->bhcqk", qq, k).astype(jnp.float32) * scale
        p = jax.nn.softmax(s, axis=-1)
        a = p[:, :, 0] - lam * p[:, :, 1]
        return jnp.einsum("bhqk,bkhe->bqhe", a.astype(v.dtype), v)

    o = lax.map(block, qb).swapaxes(0, 1).reshape(b, tq, N_HEADS, V_DIM)
    o = rmsnorm(o, subln_g) * (1.0 - lam_init)
    return o.reshape(b, tq, ATTN_WIDTH)


def trunk_layer(x, mod, lam_init, lw, rope=None, ctx_k=None, ctx_v=None):
    sh1, sc1, gt1, sh2, sc2, gt2 = jnp.split(mod, 6, axis=-1)
    b, t = x.shape[0], x.shape[1]
    h = rmsnorm(x, lw["g_pre_mix"]) * (1 + sc1) + sh1
    bg, cg, xin, q, k, v, cf_u, ga, gb, gc = split_in(h @ lw["w_in"])
    y_a = (bg * dwconv(cg * xin, lw["sc_conv_w"])) @ lw["sc_w_out"]
    q = q.reshape(b, t, N_HEADS, 2, QK_DIM)
    k = k.reshape(b, t, N_HEADS, 2, QK_DIM)
    v = v.reshape(b, t, N_HEADS, V_DIM)
    if rope is not None:
        q = axial_rope(q, rope[0], rope[1])
        k = axial_rope(k, rope[0], rope[1])
    if ctx_k is not None:
        k_all = jnp.concatenate([ctx_k, k], axis=1)
        v_all = jnp.concatenate([ctx_v, v], axis=1)
    else:
        k_all, v_all = k, v
    f32 = jnp.float32
    lam = (jnp.exp(jnp.sum(lw["lam_q1"].astype(f32) * lw["lam_k1"].astype(f32)))
           - jnp.exp(jnp.sum(lw["lam_q2"].astype(f32) * lw["lam_k2"].astype(f32))) + lam_init)
    y_b = diff_attention(q, k_all, v_all, lam, lam_init, lw["subln_g"]) @ lw["attn_w_out"]
    za, zg = jnp.split(cf_u, 2, axis=-1)
    z = dwconv(za * jax.nn.sigmoid(zg), lw["cf_conv_w"]) + lw["cf_conv_b"]
    z = jax.nn.silu(layernorm(z, lw["cf_ln_g"], lw["cf_ln_b"]))
    y_c = z @ lw["cf_w_out"] + lw["cf_b_out"]
    m = jax.nn.sigmoid(ga) * y_a + jax.nn.sigmoid(gb) * y_b + jax.nn.sigmoid(gc) * y_c
    x = x + gt1 * rmsnorm(m @ lw["w_o"], lw["g_post_mix"])
    h = rmsnorm(x, lw["g_pre_mlp"]) * (1 + sc2) + sh2
    f = jnp.square(jax.nn.relu(h @ lw["w_ff1"])) @ lw["w_ff2"]
    x = x + gt2 * rmsnorm(f, lw["g_post_mlp"])
    return x, k, v


def setup_inputs(seed: int = 0) -> dict:
    key = jax.random.key(seed)
    ks = jax.random.split(key, 40)

    def nrm(i, shape, scale):
        return jax.random.normal(ks[i], shape, jnp.float32) * scale

    def gain(i, shape):
        return 1.0 + 0.05 * jax.random.normal(ks[i], shape, jnp.float32)

    D, L = D_MODEL, DEPTH
    return {
        "x_prompt": nrm(0, (BATCH, SEQ, D), 1.0),
        "x_sample": nrm(1, (DEC_BATCH, DEC_SEQ, D), 1.0),
        "cache_k": nrm(2, (DEC_BATCH, L, PAST_LEN, N_HEADS, 2, QK_DIM), 1.0),
        "cache_v": nrm(3, (DEC_BATCH, L, PAST_LEN, N_HEADS, V_DIM), 1.0),
        "c": nrm(4, (DEC_BATCH, D), 1.0),
        "c_ctx": nrm(5, (D,), 1.0),
        "w_mod": nrm(6, (L, D, 6 * D), 0.5 * D ** -0.5),
        "b_mod": nrm(7, (L, 6 * D), 0.02),
        "g_pre_mix": gain(8, (L, D)),
        "g_post_mix": gain(9, (L, D)),
        "g_pre_mlp": gain(10, (L, D)),
        "g_post_mlp": gain(11, (L, D)),
        "w_in": nrm(12, (L, D, IN_COLS), D ** -0.5),
        "sc_conv_w": nrm(13, (L, SC_KERNEL, SC_WIDTH), SC_KERNEL ** -0.5),
        "sc_w_out": nrm(14, (L, SC_WIDTH, D), SC_WIDTH ** -0.5),
        "lam_q1": nrm(15, (L, QK_DIM), 0.1),
        "lam_k1": nrm(16, (L, QK_DIM), 0.1),
        "lam_q2": nrm(17, (L, QK_DIM), 0.1),
        "lam_k2": nrm(18, (L, QK_DIM), 0.1),
        "subln_g": gain(19, (L, V_DIM)),
        "attn_w_out": nrm(20, (L, ATTN_WIDTH, D), ATTN_WIDTH ** -0.5),
        "cf_conv_w": nrm(21, (L, CF_KERNEL, CF_WIDTH), CF_KERNEL ** -0.5),
        "cf_conv_b": nrm(22, (L, CF_WIDTH), 0.02),
        "cf_ln_g": gain(23, (L, CF_WIDTH)),
        "cf_ln_b": nrm(24, (L, CF_WIDTH), 0.02),
        "cf_w_out": nrm(25, (L, CF_WIDTH, D), CF_WIDTH ** -0.5),
        "cf_b_out": nrm(26, (L, D), 0.02),
        "w_o": nrm(27, (L, D, D), D ** -0.5),
        "w_ff1": nrm(28, (L, D, D_FF), D ** -0.5),
        "w_ff2": nrm(29, (L, D_FF, D), D_FF ** -0.5),
    }


def reference(x_prompt, x_sample, cache_k, cache_v, c, c_ctx, w_mod, b_mod,
              g_pre_mix, g_post_mix, g_pre_mlp, g_post_mlp, w_in, sc_conv_w, sc_w_out,
              lam_q1, lam_k1, lam_q2, lam_k2, subln_g, attn_w_out,
              cf_conv_w, cf_conv_b, cf_ln_g, cf_ln_b, cf_w_out, cf_b_out,
              w_o, w_ff1, w_ff2):
    t_lat = x_sample.shape[1]
    rows = t_lat // GRID_W
    row_idx = jnp.repeat(jnp.arange(rows, dtype=jnp.int32), GRID_W)
    col_idx = jnp.tile(jnp.arange(GRID_W, dtype=jnp.int32), rows)
    y_p, y_s = x_prompt, x_sample
    new_k, new_v = [], []
    for l in range(DEPTH):
        lw = dict(g_pre_mix=g_pre_mix[l], g_post_mix=g_post_mix[l],
                  g_pre_mlp=g_pre_mlp[l], g_post_mlp=g_post_mlp[l],
                  w_in=w_in[l], sc_conv_w=sc_conv_w[l], sc_w_out=sc_w_out[l],
                  lam_q1=lam_q1[l], lam_k1=lam_k1[l], lam_q2=lam_q2[l], lam_k2=lam_k2[l],
                  subln_g=subln_g[l], attn_w_out=attn_w_out[l],
                  cf_conv_w=cf_conv_w[l], cf_conv_b=cf_conv_b[l], cf_ln_g=cf_ln_g[l],
                  cf_ln_b=cf_ln_b[l], cf_w_out=cf_w_out[l], cf_b_out=cf_b_out[l],
                  w_o=w_o[l], w_ff1=w_ff1[l], w_ff2=w_ff2[l])
        lam_init = 0.8 - 0.6 * math.exp(-0.3 * l)
        mod_ctx = (jax.nn.silu(c_ctx) @ w_mod[l] + b_mod[l])[None, None, :]
        mod_lat = (jax.nn.silu(c) @ w_mod[l] + b_mod[l])[:, None, :]
        y_p, k_l, v_l = trunk_layer(y_p, mod_ctx, lam_init, lw)
        new_k.append(k_l)
        new_v.append(v_l)
        y_s, _, _ = trunk_layer(y_s, mod_lat, lam_init, lw, rope=(row_idx, col_idx),
                                ctx_k=cache_k[:, l], ctx_v=cache_v[:, l])
    new_cache_k = jnp.stack(new_k, axis=1)
    new_cache_v = jnp.stack(new_v, axis=1)
    return (y_p, y_s, new_cache_k, new_cache_v)
```

```python
import math
from collections import deque
from contextlib import ExitStack

import numpy as np
import concourse.bass as bass
import concourse.mybir as mybir
from concourse.bass_utils import run_bass_kernel_spmd

F32 = mybir.dt.float32
BF16 = mybir.dt.bfloat16
AF = mybir.ActivationFunctionType
ALU = mybir.AluOpType

D = 1024
T = 512
L = 2
H = 8
NCOL = 8704
DFF = 4096
NPT = 2
NST = 4
EPS = 1e-6
C_BG, C_CG, C_XIN, C_Q, C_K, C_V, C_ZA, C_ZG, C_GA, C_GB, C_GC = 0, 4, 8, 12, 20, 28, 36, 40, 44, 52, 60
GROWS = 2056


class Buf:
    __slots__ = ("name", "w", "r", "excl")

    def __init__(self, name="", excl=False):
        self.name = name
        self.w = None
        self.r = []
        self.excl = excl


def bufs(name, n, excl=False):
    return [Buf(f"{name}{i}", excl) for i in range(n)]


class Prog:
    ENGS = ("pe", "act", "dve", "pool", "sp")

    def __init__(self, nc, stack):
        self.nc = nc
        self.stack = stack
        self.ops = {e: [] for e in self.ENGS}
        self.sems = {}
        self.cnt = {}
        self.seen = {e: {} for e in self.ENGS}
        self.trace_ops = False
        for e in self.ENGS:
            self.newsem("E_" + e)

    def newsem(self, key):
        self.sems[key] = self.stack.enter_context(self.nc.semaphore(key))
        self.cnt[key] = 0
        return key

    def _waits(self, eng, reads, writes):
        need = {}

        def add(tok):
            if tok is None:
                return
            k, v = tok
            if need.get(k, 0) < v:
                need[k] = v
        for b in reads:
            add(b.w)
            if b.excl:
                for t in b.r:
                    add(t)
        for b in writes:
            add(b.w)
            for t in b.r:
                add(t)
        out = []
        seen = self.seen[eng]
        for k, v in need.items():
            if k == "E_pe" and eng == "pe":
                continue
            if not k.startswith("E_"):
                v = self.cnt[k]
            if seen.get(k, 0) < v:
                seen[k] = v
                out.append((k, v))
        return out

    def _commit(self, tok, reads, writes):
        for b in reads:
            b.r.append(tok)
        for b in writes:
            b.w = tok
            b.r = []

    def op(self, eng, fn, reads=(), writes=()):
        waits = self._waits(eng, reads, writes)
        key = "E_" + eng
        self.cnt[key] += 1
        tok = (key, self.cnt[key])
        self.ops[eng].append((waits, fn, key, 1))
        self._commit(tok, reads, writes)
        if self.trace_ops:
            import traceback
            fn._where = [f"{f.name}:{f.lineno}" for f in traceback.extract_stack(limit=6)[:-1]]
        return tok

    def dma(self, eng, fn, semkey, reads=(), writes=(), n=1, inc=16, after=()):
        waits = self._waits(eng, list(reads) + list(after), writes)
        self.cnt[semkey] += inc * n
        tok = (semkey, self.cnt[semkey])
        self.ops[eng].append((waits, fn, semkey, inc))
        self._commit(tok, reads, writes)
        if self.trace_ops:
            import traceback
            fn._where = [f"{f.name}:{f.lineno}" for f in traceback.extract_stack(limit=6)[:-1]]
        return tok

    def wait_all(self, eng, toks):
        waits = []
        for k, v in toks:
            if self.seen[eng].get(k, 0) < v:
                self.seen[eng][k] = v
                waits.append((k, v))
        self.ops[eng].append((waits, None, None, 0))

    def emit(self, eng, e):
        sems = self.sems
        for waits, fn, key, inc in self.ops[eng]:
            for k, v in waits:
                e.wait_ge(sems[k], v)
            if fn is None:
                continue
            try:
                r = fn(e)
            except Exception:
                print("FAILED OP at", getattr(fn, "_where", None))
                raise
            if isinstance(r, (list, tuple)):
                for i in r:
                    if inc == 1:
                        i.then_inc(sems[key])
                    else:
                        i.then_inc(sems[key], inc)
            else:
                r.then_inc(sems[key], inc)

    def run_block(self):
        nc = self.nc
        with nc.Block() as block:
            @block.tensor
            def _(e):
                self.emit("pe", e)

            @block.scalar
            def _(e):
                self.emit("act", e)

            @block.vector
            def _(e):
                self.emit("dve", e)

            @block.gpsimd
            def _(e):
                self.emit("pool", e)

            @block.sync
            def _(e):
                self.emit("sp", e)


IN_SHAPES = {
    "xp": [1024, D], "xs": [2048, D], "ck": [L, 512, D], "cv": [L, 512, D], "cvec": [2, D],
    "w_mod": [L, D, 6 * D], "b_mod": [L, 6 * D],
    "g_pre_mix": [L, D], "g_post_mix": [L, D], "g_pre_mlp": [L, D], "g_post_mlp": [L, D],
    "w_in": [L, D, NCOL], "sc_conv_w": [L, 3, 512], "sc_w_out": [L, 512, D],
    "lam_q1": [L, 64], "lam_k1": [L, 64], "lam_q2": [L, 64], "lam_k2": [L, 64],
    "subln_g": [L, 128], "attn_w_out": [L, D, D], "cf_conv_w": [L, 31, 512],
    "cf_conv_b": [L, 512], "cf_ln_g": [L, 512], "cf_ln_b": [L, 512],
    "cf_w_out": [L, 512, D], "cf_b_out": [L, D], "w_o": [L, D, D],
    "w_ff1": [L, D, DFF], "w_ff2": [L, DFF, D],
    "rope_c": [128, 2048], "rope_s": [128, 2048], "perm": [128, 128], "ident": [128, 128],
    "masks": [128, 2],
}
OUT_SHAPES = {"yp": [1024, D], "ys": [2048, D], "nk": [4, L, 256, D], "nv": [4, L, 256, D]}
BIGW = {"w_in": (D, NCOL), "sc_w_out": (512, D), "attn_w_out": (D, D), "cf_w_out": (512, D),
        "w_o": (D, D), "w_ff1": (D, DFF), "w_ff2": (DFF, D)}


class Builder:
    def __init__(self, nlayers=L, do_sample=True, do_prompt=True, dbg=()):
        self.nlayers = nlayers
        self.do_sample = do_sample
        self.do_prompt = do_prompt
        self.dbg = dict(dbg)
        self.trace_ops = False
        self.stage = 99
        self.ncores = 8
        self.sub = 0
        self.nc = bass.Bass("TRN2", target_bir_lowering=False)
        nc = self.nc
        self.I = {k: nc.dram_tensor(k, s, F32, kind="ExternalInput").ap() for k, s in IN_SHAPES.items()}
        self.O = {k: nc.dram_tensor(k, s, F32, kind="ExternalOutput").ap() for k, s in OUT_SHAPES.items()}
        self.DBG = {k: nc.dram_tensor("dbg_" + k, s, F32, kind="ExternalOutput").ap() for k, s in self.dbg.items()}
        self.W16 = {k: nc.dram_tensor(k + "_16", [L, r, c], BF16).ap() for k, (r, c) in BIGW.items()}
        self.XS = nc.dram_tensor("xs_scr", [NPT + NST, 128, 8, T], F32).ap()
        self.GINP = [nc.dram_tensor(f"gin{i}", [512 if i < 4 else 8, 2048], BF16).ap() for i in range(5)]
        self.GOUTP = [nc.dram_tensor(f"gout{i}", [1024 if i < 4 else 16, 2048], BF16).ap() for i in range(5)]
        self.KCTX = nc.dram_tensor("kctx", [128, H, 512], BF16).ap()
        self.VCTX = nc.dram_tensor("vctx", [H, 128, 4, 128], BF16).ap()
        self.SUS = nc.dram_tensor("sus", [128, 4, 2050], BF16).ap()
        self.SUU = nc.dram_tensor("suu", [128, 4, 2078], BF16).ap()
        self.MODS = nc.dram_tensor("mods", [L, 2, 6 * D], F32).ap()
        self.WM16 = nc.dram_tensor("wm16", [L, D, 6 * D], BF16).ap()

    def sb(self, name, shape, dt):
        return self.st.enter_context(self.nc.sbuf_tensor("s_" + name, list(shape), dt))

    def act(self, out, in_, func, reads, writes, bias=None, scale=None):
        kw = {}
        if bias is not None:
            kw["bias"] = bias
        if scale is not None:
            kw["scale"] = scale
        return self.P.op("act", lambda e: e.activation(out=out, in_=in_, func=func, **kw), reads, writes)

    def tt(self, eng, out, in0, in1, op, reads, writes):
        return self.P.op(eng, lambda e: e.tensor_tensor(out=out, in0=in0, in1=in1, op=op), reads, writes)

    def ts(self, eng, out, in0, s1, op0, reads, writes, s2=None, op1=None):
        if s2 is None:
            return self.P.op(eng, lambda e: e.tensor_scalar(out=out, in0=in0, scalar1=s1, scalar2=None, op0=op0),
                             reads, writes)
        return self.P.op(eng, lambda e: e.tensor_scalar(out=out, in0=in0, scalar1=s1, scalar2=s2, op0=op0, op1=op1),
                         reads, writes)

    def stt(self, eng, out, in0, scalar, in1, op0, op1, reads, writes):
        return self.P.op(eng, lambda e: e.scalar_tensor_tensor(out=out, in0=in0, scalar=scalar, in1=in1,
                                                               op0=op0, op1=op1), reads, writes)

    def copy(self, eng, out, in_, reads, writes):
        if eng == "act":
            return self.P.op("act", lambda e: e.copy(out=out, in_=in_), reads, writes)
        return self.P.op(eng, lambda e: e.tensor_copy(out=out, in_=in_), reads, writes)

    def mm(self, ps_ap, pairs, reads, writes, start=True, stop=True, tile_pos=None):
        pairs = list(pairs)
        n = len(pairs)

        def fn(e):
            r = None
            for i, (lhsT, rhs) in enumerate(pairs):
                kw = {}
                if tile_pos is not None:
                    kw["tile_position"] = tile_pos
                r = e.matmul(ps_ap, lhsT=lhsT, rhs=rhs, start=(start and i == 0), stop=(stop and i == n - 1), **kw)
            return r
        return self.P.op("pe", fn, reads, writes)

    def dma(self, eng, out, in_, sem, reads, writes, slow=False):
        if slow:
            return self.P.dma(eng, lambda e: [e.dma_start(out=out, in_=in_, allow_slow_non_contiguous=True)], sem,
                              reads, writes)
        return self.P.dma(eng, lambda e: [e.dma_start(out=out, in_=in_)], sem, reads, writes)

    def ps_get(self):
        i = self.ps_free.popleft()
        return i, self.PS[i], self.PSB[i]

    def ps_put(self, i):
        self.ps_free.append(i)

    def tmp(self):
        i = self.tmp_i
        self.tmp_i = (i + 1) % self.NTMP
        return self.TMP[:, i, :], self.TMPB[i]

    def sq(self):
        i = self.sq_i
        self.sq_i = (i + 1) % 2
        return self.SQ[:, i, :], self.SQB[i]

    def dump(self, name, src_ap, src_bufs):
        if name in self.DBG and not self.dumped.get(name):
            self.dumped[name] = True
            self.dma("pool", self.DBG[name], src_ap, "dbg", src_bufs, [])

    def alloc(self):
        nc, sb = self.nc, self.sb
        P = self.P
        self.XA = sb("XA", [128, 8, T], F32); self.XAB = bufs("XA", 8)
        self.XB = sb("XB", [128, 8, T], F32); self.XBB = bufs("XB", 8)
        self.Hh = sb("Hh", [128, 8, T], BF16); self.HB = bufs("H", 8)
        self.NWR = 3
        self.WR = sb("WR", [128, self.NWR, 8, T], BF16); self.WRB = bufs("WR", self.NWR)
        self.wr_i = 0
        self.R = sb("R", [128, 36 * 512], BF16); self.RB = bufs("R", 36)
        self.QM = sb("QM", [128, 8, T], BF16); self.QMB = bufs("QM", 8)
        self.BG = sb("BG", [128, 4, T], BF16); self.BGB = bufs("BG", 4)
        self.SW = sb("SW", [128, 4, 516], BF16); self.SWB = bufs("SW", 4)
        self.UW = sb("UW", [128, 4, 572], BF16); self.UWB = bufs("UW", 4)
        self.AIN = sb("AIN", [128, 4, T], BF16); self.AINB = bufs("AIN", 4)
        self.ZS = sb("ZS", [128, 4, T], BF16); self.ZSB = bufs("ZS", 4)
        self.OT = sb("OT", [128, 8, T], BF16); self.OTB = bufs("OT", 8)
        self.E = sb("E", [128, 6, T], BF16); self.EB = bufs("E", 6)
        self.ZACC = sb("ZACC", [128, 2, 2, T], F32); self.ZBS = bufs("ZACC", 2); self.z_i = 0
        self.ones_f = sb("ones_f", [128, 128], F32)
        self.ROPE = sb("ROPE", [128, 2, T], F32); self.ROPEB = Buf("ROPE")
        self.DG3 = sb("DG3", [128, 4, 3, 128], BF16); self.DG3B = Buf("DG3")
        self.DG31 = sb("DG31", [128, 1, 31, 128], BF16); self.DG31B = bufs("DG31", 1)
        self.NTMP = 10
        self.TMP = sb("TMP", [128, self.NTMP, T], F32); self.TMPB = bufs("TMP", self.NTMP); self.tmp_i = 0
        self.SQ = sb("SQ", [128, 2, T], BF16); self.SQB = bufs("SQ", 2); self.sq_i = 0
        self.RSTD = sb("RSTD", [128, T], F32); self.RSTDB = Buf("RSTD")
        self.XT = sb("XT", [128, 2, D], F32); self.XTB = bufs("XT", 2); self.xt_i = 0
        self.STG = self.XT; self.STGB = self.XTB; self.stg_i = 0
        self.ident = sb("ident", [128, 128], F32)
        self.ident_bf = sb("ident_bf", [128, 128], BF16)
        self.ones_bf = sb("ones_bf", [128, 128], BF16)
        self.perm_bf = sb("perm_bf", [128, 128], BF16)
        self.eps_col = sb("eps_col", [128, 1], F32)
        self.masks = sb("masks", [128, 2], F32)
        self.CONSTB = Buf("CONST")
        self.VCOL = sb("VCOL", [128, 4, 128], F32)
        A = self.VCOL[:, 0, :]
        self.vD = {k: A[:, 16 * i:16 * i + 16].rearrange("p (l j) -> p l j", l=L) for i, k in
                   enumerate(("g_pre_mix", "g_post_mix", "g_pre_mlp", "g_post_mlp", "cf_b_out"))}
        self.v5 = {k: A[:, 80 + 8 * i:80 + 8 * i + 8].rearrange("p (l j) -> p l j", l=L) for i, k in
                   enumerate(("cf_conv_b", "cf_ln_g", "cf_ln_b"))}
        self.wsc = A[:, 104:128].rearrange("p (l j k) -> p l j k", l=L, j=4)
        self.wcf_l = [self.VCOL[:, 1 + l, 0:124].rearrange("p (j k) -> p j k", j=4) for l in range(L)]
        self.sublng = self.VCOL[:, 3, 16:18]
        self.cvT = self.VCOL[:, 3, 0:16].rearrange("p (g j) -> p j g", g=2)
        self.lamv = sb("lamv", [128, 4, L, 64], F32)
        self.lamt = sb("lamt", [128, 8, L], F32)
        self.VECB = Buf("VEC")
        self.scv = sb("scv", [128, 8, 2], BF16)
        self.modT = sb("modT", [128, L, 2, 48], F32); self.MODTB = bufs("modT", L)
        self.coef = sb("coef", [128, L, 2, 6, 8], F32); self.COEFB = bufs("coef", L)
        self.bmod2 = sb("bmod2", [2, T], F32); self.BMODB = Buf("bmod2")
        self.hal = sb("hal", [128, 2, 128], BF16); self.HALB = Buf("hal")
        self.PS = []
        self.PSB = bufs("PS", 8, excl=True)
        for i in range(8):
            self.PS.append(self.st.enter_context(nc.psum_tensor(f"ps{i}", [128, T], F32)))
        self.ps_free = deque(range(8))
        for k in (["vec", "cst", "cvm0", "cvm1", "conv", "xl", "xt0", "xt1", "stg0", "stg1", "st_xa", "st_qm", "st_ot", "st_ain", "st_zs",
                   "st_misc", "rope", "cc0", "cc1", "cc2", "cc3", "cc4", "mods", "modt", "hal", "halw", "kctx", "vctx", "dbg", "sw", "uw",
                   "st_e", "bmod"]
                  + [f"wr{i}" for i in range(self.NWR)] + [f"kv{i}" for i in range(4)]
                  + [f"cv_{k}{l}" for k in list(BIGW) + ["w_in_a", "w_in_b"] for l in range(L)] + [f"wm{l}" for l in range(L)]):
            P.newsem(k)
        self.W16B = {(k, l): Buf(f"W16{k}{l}") for k in list(BIGW) + ["w_in_a", "w_in_b"] for l in range(L)}
        self.XSB = bufs("XS", NPT + NST)
        self.GINB = bufs("GIN", 5); self.GOUTB = bufs("GOUT", 5)
        self.KCTXB = Buf("KCTX"); self.VCTXB = Buf("VCTX")
        self.SUSB = Buf("SUS"); self.SUUB = Buf("SUU")
        self.MODSB = bufs("MODS", L)
        self.WM16B = bufs("WM16", L)
        self.dumped = {}
        self.out_toks = []
        self.pending = deque()
        self.pace_i = 0

    def setup(self):
        P, I = self.P, self.I
        ident, ident_bf, ones_bf, perm_bf, eps_col = self.ident, self.ident_bf, self.ones_bf, self.perm_bf, self.eps_col
        C = [self.CONSTB]
        self.dma("sp", ident[:], I["ident"], "cst", [], C)
        self.dma("sp", self.masks[:], I["masks"], "cst", [], C)
        self.dma("pool", perm_bf[:], I["perm"], "conv", [], C)
        self.copy("dve", ident_bf[:], ident[:], C, C)
        P.op("dve", lambda e: e.memset(ones_bf[:], 1.0), [], C)
        P.op("dve", lambda e: e.memset(self.ones_f[:], 1.0), [], C)
        P.op("dve", lambda e: e.memset(eps_col[:], EPS), [], C)
        V = [self.VECB]
        RW = self.XT[:, 0, 0:512].rearrange("p (a c) -> p a c", a=4)
        RWB = [self.XTB[0]]
        P.op("dve", lambda e: e.memset(self.XT[:, 0, 0:512], 0.0), [], RWB)
        row = lambda a, r0, n: RW[r0:r0 + n, a, :]
        for i, k in enumerate(("g_pre_mix", "g_post_mix", "g_pre_mlp", "g_post_mlp", "cf_b_out")):
            self.dma("sp", row(0, 16 * i, 16), I[k].rearrange("l (j p) -> (l j) p", p=128), "xt0", [], RWB)
        for i, k in enumerate(("cf_conv_b", "cf_ln_g", "cf_ln_b")):
            self.dma("sp", row(0, 80 + 8 * i, 8), I[k].rearrange("l (j p) -> (l j) p", p=128), "xt0", [], RWB)
        for l in range(L):
            for j in range(4):
                self.dma("sp", row(0, 104 + (l * 4 + j) * 3, 3), I["sc_conv_w"][l, :, j * 128:(j + 1) * 128], "xt0", [], RWB)
                self.dma("sp", row(1 + l, j * 31, 31), I["cf_conv_w"][l, :, j * 128:(j + 1) * 128], "xt0", [], RWB)
        self.dma("sp", row(3, 0, 16), I["cvec"].rearrange("g (j p) -> (g j) p", p=128), "xt0", [], RWB)
        self.dma("sp", row(3, 16, 2), I["subln_g"], "xt0", [], RWB)
        pi, ps, pb = self.ps_get()

        def ftr(e, ps=ps):
            r = None
            for a in range(4):
                r = e.transpose(out=ps[:, a * 128:(a + 1) * 128], in_=RW[:, a, :], identity=self.ident[:])
            return r
        P.op("pe", ftr, RWB + C, [pb])
        self.copy("dve", self.VCOL[:].rearrange("p a c -> p (a c)"), ps[:], [pb], V)
        self.ps_put(pi)
        for i, k in enumerate(("lam_q1", "lam_k1", "lam_q2", "lam_k2")):
            src = bass.AP(I[k].tensor, 0, [[0, 128], [64, L], [1, 64]])
            self.dma("sp", self.lamv[:, i], src, "vec", [], V)
        lamv, lamt = self.lamv, self.lamt
        self.tt("dve", lamv[:, 0], lamv[:, 0], lamv[:, 1], ALU.mult, V, V)
        self.tt("dve", lamv[:, 2], lamv[:, 2], lamv[:, 3], ALU.mult, V, V)
        P.op("dve", lambda e: e.reduce_sum(out=lamt[:, 0, :], in_=lamv[:, 0], axis=mybir.AxisListType.X), V, V)
        P.op("dve", lambda e: e.reduce_sum(out=lamt[:, 1, :], in_=lamv[:, 2], axis=mybir.AxisListType.X), V, V)
        self.act(lamt[:, 2, :], lamt[:, 0, :], AF.Exp, V, V)
        self.act(lamt[:, 3, :], lamt[:, 1, :], AF.Exp, V, V)
        self.tt("dve", lamt[:, 4, :], lamt[:, 2, :], lamt[:, 3, :], ALU.subtract, V, V)
        for l in range(L):
            lam_init = 0.8 - 0.6 * math.exp(-0.3 * l)
            self.ts("dve", lamt[:, 4, l:l + 1], lamt[:, 4, l:l + 1], lam_init, ALU.add, V, V)
            self.ts("dve", lamt[:, 5, l:l + 1], lamt[:, 4, l:l + 1], -1.0, ALU.mult, V, V)
            self.ts("dve", lamt[:, 6, l:l + 1], self.sublng[:, l:l + 1], 1.0 - lam_init, ALU.mult, V, V)
        self.act(self.scv[:], self.cvT[:], AF.Silu, V, V)

    W_IN_PARTS = (("a", ((512, 1536), (2560, 5632))), ("b", ((0, 512), (1536, 2560), (5632, 8704))))

    def queue_conversions(self, l, names):
        for k in names:
            if k == "w_mod":
                for b in range(16):
                    def fn(e, b=b, l=l):
                        return [e.dma_start(out=self.WM16[l, b * 64:(b + 1) * 64, :],
                                            in_=self.I["w_mod"][l, b * 64:(b + 1) * 64, :])]
                    self.pending.append((("w_mod", l), f"cvm{l}", fn, self.WM16B[l]))
                continue
            r, c = BIGW[k]
            src, dst = self.I[k], self.W16[k]
            if k == "w_in":
                for part, ranges in self.W_IN_PARTS:
                    for b in range(8):
                        for (c0, c1) in ranges:
                            def fn(e, src=src, dst=dst, b=b, c0=c0, c1=c1, l=l):
                                return [e.dma_start(out=dst[l, b * 128:(b + 1) * 128, c0:c1],
                                                    in_=src[l, b * 128:(b + 1) * 128, c0:c1])]
                            self.pending.append((("w_in_" + part, l), f"cv_w_in_{part}{l}", fn,
                                                 self.W16B[("w_in_" + part, l)]))
                continue
            rb = max(16, min(r, (2 << 20) // (c * 4)))
            for b in range((r + rb - 1) // rb):
                r0, r1 = b * rb, min(r, (b + 1) * rb)

                def fn(e, src=src, dst=dst, r0=r0, r1=r1, l=l):
                    return [e.dma_start(out=dst[l, r0:r1, :], in_=src[l, r0:r1, :])]
                self.pending.append(((k, l), f"cv_{k}{l}", fn, self.W16B[(k, l)]))

    def issue_piece(self, deps=()):
        key, sem, fn, buf = self.pending.popleft()
        self.P.dma("pool", fn, sem, [], [buf], after=list(deps))

    def flush_until(self, key):
        while any(p[0] == key for p in self.pending):
            self.issue_piece()

    def pace(self, deps):
        if not self.pending:
            return
        self.pace_i += 1
        k0 = self.pending[0][0]
        rate = (2 if self.pace_i < 40 else 1) if (k0[1] == 0 or k0[0] == "w_mod") else 5
        if self.pace_i % rate == 0:
            self.issue_piece(deps)

    W_IN_PARTS = (("a", ((512, 1536), (2560, 5632))), ("b", ((0, 512), (1536, 2560), (5632, 8704))))

    def compute_mod(self, l):
        P, I = self.P, self.I
        V = [self.VECB]
        self.flush_until(("w_mod", l))
        for cb in range(12):
            slot = self.wr_i; self.wr_i = (slot + 1) % self.NWR
            wt = self.WR[:, slot]
            self.dma("sp", wt, self.WM16[l, :, cb * T:(cb + 1) * T].rearrange("(k p) c -> p k c", p=128),
                     f"wr{slot}", [self.WM16B[l]], [self.WRB[slot]])
            src = bass.AP(I["b_mod"].tensor, l * 6 * D + cb * T, [[0, 2], [1, T]])
            self.dma("sp", self.bmod2[:], src, "bmod", [], [self.BMODB])
            pi, ps, pb = self.ps_get()
            self.mm(ps[0:2, :], [(self.scv[:, kc, :], wt[:, kc, :]) for kc in range(8)], [self.WRB[slot]] + V, [pb])
            ta, tb = self.tmp()
            self.tt("dve", ta[0:2, :], ps[0:2, :], self.bmod2[:], ALU.add, [pb, self.BMODB], [tb])
            self.ps_put(pi)
            p2i, ps2, pb2 = self.ps_get()

            def ftr(e, ta=ta, ps2=ps2):
                r = None
                for q in range(4):
                    r = e.transpose(out=ps2[:, 2 * q:2 * q + 2], in_=ta[0:2, q * 128:(q + 1) * 128],
                                    identity=self.ident[0:2, 0:2])
                return r
            self.P.op("pe", ftr, [tb, self.CONSTB], [pb2])
            self.copy("dve", self.modT[:, l, :, 4 * cb:4 * cb + 4], ps2[:, 0:8].rearrange("p (q g) -> p g q", g=2),
                      [pb2], [self.MODTB[l]])
            self.ps_put(p2i)
        cf = self.coef
        rd = [self.MODTB[l]] + V
        wr = [self.COEFB[l]]
        for g in range(2):
            m = self.modT[:, l, g]
            self.stt("dve", cf[:, l, g, 0], m[:, 8:16], 1.0, self.vD["g_pre_mix"][:, l], ALU.add, ALU.mult, rd, wr)
            self.copy("dve", cf[:, l, g, 1], m[:, 0:8], rd, wr)
            self.tt("dve", cf[:, l, g, 2], m[:, 16:24], self.vD["g_post_mix"][:, l], ALU.mult, rd, wr)
            self.stt("dve", cf[:, l, g, 3], m[:, 32:40], 1.0, self.vD["g_pre_mlp"][:, l], ALU.add, ALU.mult, rd, wr)
            self.copy("dve", cf[:, l, g, 4], m[:, 24:32], rd, wr)
            self.tt("dve", cf[:, l, g, 5], m[:, 40:48], self.vD["g_post_mlp"][:, l], ALU.mult, rd, wr)

    def load_x(self, kind, ti, l):
        gi = ti if kind == "p" else NPT + ti
        if l > 0:
            self.dma("sp", self.XA[:], self.XS[gi], "xl", [self.XSB[gi]], self.XAB)
            return
        src = self.I["xp"] if kind == "p" else self.I["xs"]
        for tb in range(4):
            xi = self.xt_i; self.xt_i = 1 - xi
            r0 = ti * T + tb * 128
            self.dma("sp", self.XT[:, xi, :], src[r0:r0 + 128, :], f"xt{xi}", [], [self.XTB[xi]])
            for g in range(2):
                pi, ps, pb = self.ps_get()
                xt = self.XT

                def fn(e, xi=xi, g=g, ps=ps):
                    r = None
                    for q in range(4):
                        kc = g * 4 + q
                        r = e.transpose(out=ps[:, q * 128:(q + 1) * 128], in_=xt[:, xi, kc * 128:(kc + 1) * 128],
                                        identity=self.ident[:])
                    return r
                self.P.op("pe", fn, [self.XTB[xi], self.CONSTB], [pb])
                eng = "act" if g == 0 else "dve"
                self.copy(eng, self.XA[:, g * 4:(g + 1) * 4, tb * 128:(tb + 1) * 128],
                          ps[:].rearrange("p (q t) -> p q t", q=4), [pb], self.XAB[g * 4:(g + 1) * 4])
                self.ps_put(pi)

    def store_x(self, kind, ti, l):
        gi = ti if kind == "p" else NPT + ti
        if l < self.nlayers - 1:
            self.dma("act", self.XS[gi], self.XA[:], "st_xa", self.XAB, [self.XSB[gi]])
            return
        dst = self.O["yp"] if kind == "p" else self.O["ys"]
        XA = self.XA
        for tb in range(4):
            si = self.stg_i; self.stg_i = 1 - si
            for g in range(2):
                pi, ps, pb = self.ps_get()

                def fn(e, g=g, ps=ps, tb=tb):
                    r = None
                    for q in range(4):
                        kc = g * 4 + q
                        r = e.transpose(out=ps[:, q * 128:(q + 1) * 128], in_=XA[:, kc, tb * 128:(tb + 1) * 128],
                                        identity=self.ident[:])
                    return r
                self.P.op("pe", fn, self.XAB[g * 4:(g + 1) * 4] + [self.CONSTB], [pb])
                eng = "act" if g == 0 else "dve"
                self.copy(eng, self.STG[:, si, g * T:(g + 1) * T], ps[:], [pb], [self.STGB[si]])
                self.ps_put(pi)
            r0 = ti * T + tb * 128
            tok = self.dma("act", dst[r0:r0 + 128, :], self.STG[:, si, :], f"stg{si}", [self.STGB[si]], [])
            self.out_toks.append(tok)

    def stats_add(self, ps, pb, src, src_bufs, i, n):
        sq, sqb = self.sq()
        sq = sq[:, :src.shape[-1]]
        self.act(sq, src, AF.Square, src_bufs, [sqb])
        self.mm(ps, [(self.ones_bf[:], sq)], [sqb, self.CONSTB], [pb], start=(i == 0), stop=(i == n - 1))

    def stats_rstd(self, ps, pb, nfeat, nt=T):
        ta, tb = self.tmp()
        self.act(ta[:, :nt], ps[:, :nt], AF.Ln, [pb, self.CONSTB], [tb], bias=self.eps_col[:], scale=1.0 / nfeat)
        self.act(self.RSTD[:, :nt], ta[:, :nt], AF.Exp, [tb], [self.RSTDB], scale=-0.5)

    def norm_mod(self, l, g, ia, ib):
        pi, ps, pb = self.ps_get()
        for kc in range(8):
            if kc % 2 == 0:
                self.stats_add(ps[:], pb, self.XA[:, kc, :], [self.XAB[kc]], kc, 8)
            else:
                sq, sqb = self.sq()
                self.tt("pool", sq, self.XA[:, kc, :], self.XA[:, kc, :], ALU.mult, [self.XAB[kc]], [sqb])
                self.mm(ps[:], [(self.ones_bf[:], sq)], [sqb, self.CONSTB], [pb], start=False, stop=(kc == 7))
        self.stats_rstd(ps, pb, D)
        self.ps_put(pi)
        cf = self.coef
        for kc in range(8):
            ta, tb = self.tmp()
            self.stt("dve", ta, self.XA[:, kc, :], cf[:, l, g, ia, kc:kc + 1], self.RSTD[:],
                     ALU.mult, ALU.mult, [self.XAB[kc], self.RSTDB, self.COEFB[l]], [tb])
            self.act(self.Hh[:, kc, :], ta, AF.Identity, [tb, self.COEFB[l]], [self.HB[kc]],
                     bias=cf[:, l, g, ib, kc:kc + 1])

    def post_norm_residual(self, l, g, ig):
        cf = self.coef
        for j in range(8):
            ta, tb = self.tmp()
            self.stt("dve", ta, self.XB[:, j, :], cf[:, l, g, ig, j:j + 1], self.RSTD[:], ALU.mult, ALU.mult,
                     [self.XBB[j], self.RSTDB, self.COEFB[l]], [tb])
            self.tt("dve", self.XA[:, j, :], self.XA[:, j, :], ta, ALU.add, [tb, self.XAB[j]], [self.XAB[j]])

    def load_slab(self, wname, l, k0, kcn, c0, w):
        slot = self.wr_i; self.wr_i = (slot + 1) % self.NWR
        W = self.W16[wname]
        key = wname
        if wname == "w_in":
            key = "w_in_a" if (512 <= c0 < 1536 or 2560 <= c0 < 5632) else "w_in_b"
        self.flush_until((key, l))
        self.dma("sp", self.WR[:, slot, 0:kcn, 0:w],
                 W[l, k0 * 128:(k0 + kcn) * 128, c0:c0 + w].rearrange("(k p) c -> p k c", p=128),
                 f"wr{slot}", [self.W16B[(key, l)]], [self.WRB[slot]])
        self.pace([self.WRB[slot]])
        return slot

    def proj_fm(self, wname, l, c0, nchunks, rhs, handler):
        kcn = len(rhs)
        j = 0
        while j < nchunks:
            nj = min(4, nchunks - j)
            if kcn <= 8:
                slot = self.load_slab(wname, l, 0, kcn, c0 + j * 128, nj * 128)
                for q in range(nj):
                    pi, ps, pb = self.ps_get()
                    self.mm(ps[:], [(self.WR[:, slot, kc, q * 128:(q + 1) * 128], rhs[kc][0]) for kc in range(kcn)],
                            [self.WRB[slot]] + [r[1] for r in rhs], [pb])
                    handler(j + q, ps, pb)
                    self.ps_put(pi)
            else:
                nks = kcn // 8
                accs = [self.ps_get() for _ in range(nj)]
                for ks in range(nks):
                    slot = self.load_slab(wname, l, ks * 8, 8, c0 + j * 128, nj * 128)
                    for q in range(nj):
                        pi, ps, pb = accs[q]
                        self.mm(ps[:], [(self.WR[:, slot, kc, q * 128:(q + 1) * 128], rhs[ks * 8 + kc][0])
                                        for kc in range(8)],
                                [self.WRB[slot]] + [rhs[ks * 8 + kc][1] for kc in range(8)], [pb],
                                start=(ks == 0), stop=(ks == nks - 1))
                for q in range(nj):
                    pi, ps, pb = accs[q]
                    handler(j + q, ps, pb)
                    self.ps_put(pi)
            j += nj

    def proj_tm(self, l, c0, handler):
        slot = self.load_slab("w_in", l, 0, 8, c0, T)
        for tb in range(4):
            pi, ps, pb = self.ps_get()
            self.mm(ps[:], [(self.Hh[:, kc, tb * 128:(tb + 1) * 128], self.WR[:, slot, kc, :]) for kc in range(8)],
                    [self.WRB[slot]] + self.HB, [pb])
            handler(tb, ps, pb)
            self.ps_put(pi)

    def hchunks(self):
        return [(self.Hh[:, kc, :], self.HB[kc]) for kc in range(8)]

    def rope_to(self, ps, pb, out_ap, out_bufs):
        qb, qbb = self.sq()
        self.copy("act", qb, ps[:], [pb], [qbb])
        p2i, ps2, pb2 = self.ps_get()
        self.mm(ps2[:], [(self.perm_bf[:], qb)], [qbb, self.CONSTB], [pb2])
        t1, t1b = self.tmp()
        t2, t2b = self.tmp()
        self.tt("dve", t1, ps[:], self.ROPE[:, 0, :], ALU.mult, [pb, self.ROPEB], [t1b])
        self.tt("dve", t2, ps2[:], self.ROPE[:, 1, :], ALU.mult, [pb2, self.ROPEB], [t2b])
        self.ps_put(p2i)
        self.tt("dve", out_ap, t1, t2, ALU.add, [t1b, t2b], out_bufs)

    def load_rope(self, ti):
        self.dma("sp", self.ROPE[:, 0, :], self.I["rope_c"][:, ti * T:(ti + 1) * T], "rope", [], [self.ROPEB])
        self.dma("sp", self.ROPE[:, 1, :], self.I["rope_s"][:, ti * T:(ti + 1) * T], "rope", [], [self.ROPEB])

    def front_gated_inputs(self, l, s_out, s_bufs, u_out, u_bufs):
        H_ = self.hchunks()

        def h_cg(j, ps, pb):
            self.copy("act", self.XB[:, j, :], ps[:], [pb], [self.XBB[j]])
        self.proj_fm("w_in", l, C_CG * 128, 4, H_, h_cg)

        def h_xin(j, ps, pb):
            o = s_out(j)
            i0 = ps[:] if len(o.shape) == 2 else ps[:].rearrange("p (s t) -> p s t", s=2)
            i1 = self.XB[:, j, :] if len(o.shape) == 2 else self.XB[:, j, :].rearrange("p (s t) -> p s t", s=2)
            self.tt("dve", o, i0, i1, ALU.mult, [pb, self.XBB[j]], [s_bufs[j]])
        self.proj_fm("w_in", l, C_XIN * 128, 4, H_, h_xin)

        def h_zg(j, ps, pb):
            self.act(self.XB[:, 4 + j, :], ps[:], AF.Sigmoid, [pb], [self.XBB[4 + j]])
        self.proj_fm("w_in", l, C_ZG * 128, 4, H_, h_zg)

        def h_za(j, ps, pb):
            o = u_out(j)
            i0 = ps[:] if len(o.shape) == 2 else ps[:].rearrange("p (s t) -> p s t", s=2)
            i1 = self.XB[:, 4 + j, :] if len(o.shape) == 2 else self.XB[:, 4 + j, :].rearrange("p (s t) -> p s t", s=2)
            self.tt("dve", o, i0, i1, ALU.mult, [pb, self.XBB[4 + j]], [u_bufs[j]])
        self.proj_fm("w_in", l, C_ZA * 128, 4, H_, h_za)

    def front_bg(self, l):
        def h_bg(j, ps, pb):
            self.copy("act", self.BG[:, j, :], ps[:], [pb], [self.BGB[j]])
        self.proj_fm("w_in", l, C_BG * 128, 4, self.hchunks(), h_bg)

    def front_prompt(self, l, ti):
        P = self.P
        P.op("pool", lambda e: e.memset(self.SW[:], 0.0), [], self.SWB)
        P.op("pool", lambda e: e.memset(self.UW[:], 0.0), [], self.UWB)
        sview = lambda j: self.SW[:, j, 0:516].rearrange("p (s t) -> p s t", s=2)[:, :, 1:257]
        uview = lambda j: self.UW[:, j, 0:572].rearrange("p (s t) -> p s t", s=2)[:, :, 15:271]
        self.front_gated_inputs(l, sview, self.SWB, uview, self.UWB)
        if self.stage == 4 and self.sub == 1:
            return
        self.front_bg(l)
        H_ = self.hchunks()

        def h_q(h, ps, pb):
            self.copy("dve", self.QM[:, h, :], ps[:], [pb], [self.QMB[h]])
        self.proj_fm("w_in", l, C_Q * 128, 8, H_, h_q)

        def h_k(h, ps, pb):
            self.copy("act", self.R[:, h * T:(h + 1) * T], ps[:], [pb], [self.RB[h]])
        self.proj_fm("w_in", l, C_K * 128, 8, H_, h_k)
        if self.stage == 4 and self.sub == 2:
            return
        for which, c0, dst in (("v", C_V * 128, self.O["nv"]), ("k", C_K * 128, self.O["nk"])):
            for hf in range(2):
                def h_tm(tb, ps, pb, hf=hf, which=which, dst=dst):
                    ta, tbb = self.tmp()
                    ti_ = self.TMPB.index(tbb)
                    self.copy("act", ta, ps[:], [pb], [tbb])
                    if which == "v":
                        u = 8 + 2 * tb + hf
                        self.copy("dve", self.R[:, u * T:(u + 1) * T], ta, [tbb], [self.RB[u]])
                    b = ti * 2 + tb // 2
                    r0 = (tb % 2) * 128
                    if self.sub != 5:
                        tok = self.dma("act", dst[b, l, r0:r0 + 128, hf * T:(hf + 1) * T], ta, f"st_tmp{ti_}", [tbb], [])
                        self.out_toks.append(tok)
                self.proj_tm(l, c0 + hf * T, h_tm)

    def front_sample_a(self, l, ti):
        self.load_rope(ti)
        self.front_gated_inputs(l, lambda j: self.AIN[:, j, :], self.AINB, lambda j: self.ZS[:, j, :], self.ZSB)
        t0 = ti * T
        self.dma("act", self.SUS[:, :, 1 + t0:1 + t0 + T], self.AIN[:], "st_ain", self.AINB, [self.SUSB])
        self.dma("act", self.SUU[:, :, 15 + t0:15 + t0 + T], self.ZS[:], "st_zs", self.ZSB, [self.SUUB])
        halo = self.GINP[4].rearrange("r (q e) -> (r q) e", e=128)
        if ti == 0:
            self.dma("act", halo[:, 0:60].rearrange("p (c k) -> p c k", k=15), self.ZS[:, :, 0:15], "st_zs",
                     self.ZSB, [self.GINB[4]])
            self.dma("act", halo[:, 120:124].rearrange("p (c k) -> p c k", k=1), self.AIN[:, :, 0:1], "st_ain",
                     self.AINB, [self.GINB[4]], slow=True)
        if ti == NST - 1:
            self.dma("act", halo[:, 60:120].rearrange("p (c k) -> p c k", k=15), self.ZS[:, :, T - 15:T], "st_zs",
                     self.ZSB, [self.GINB[4]])
            self.dma("act", halo[:, 124:128].rearrange("p (c k) -> p c k", k=1), self.AIN[:, :, T - 1:T], "st_ain",
                     self.AINB, [self.GINB[4]], slow=True)
        H_ = self.hchunks()

        def h_k(h, ps, pb):
            self.rope_to(ps, pb, self.QM[:, h, :], [self.QMB[h]])
        self.proj_fm("w_in", l, C_K * 128, 8, H_, h_k)
        for i in range(2):
            self.dma("act", self.GINP[i][:, t0:t0 + T].rearrange("(h p) t -> p h t", p=128),
                     self.QM[:, 4 * i:4 * i + 4, :], "st_qm", self.QMB[4 * i:4 * i + 4], [self.GINB[i]])
        for hf in range(2):
            def h_v(tb, ps, pb, hf=hf):
                self.copy("act" if tb % 2 else "dve", self.OT[:, tb * 2 + hf, :], ps[:], [pb], [self.OTB[tb * 2 + hf]])
            self.proj_tm(l, C_V * 128 + hf * T, h_v)
        for tb in range(4):
            kcg = ti * 4 + tb
            for i in range(2):
                self.dma("act", self.GINP[2 + i][:, kcg * 128:(kcg + 1) * 128].rearrange("(h p) v -> p h v", p=128),
                         self.OT[:, tb * 2 + i, :].rearrange("p (b v) -> p b v", v=128), "st_ot",
                         [self.OTB[tb * 2 + i]], [self.GINB[2 + i]])

    def front_sample_b(self, l, ti):
        self.load_rope(ti)
        t0 = ti * T
        self.dma("sp", self.SW[:, :, 0:514], self.SUS[:, :, t0:t0 + 514], "sw", [self.SUSB], self.SWB)
        self.dma("sp", self.UW[:, :, 0:542], self.SUU[:, :, t0:t0 + 542], "uw", [self.SUUB], self.UWB)
        self.front_bg(l)

        def h_q(h, ps, pb):
            self.rope_to(ps, pb, self.QM[:, h, :], [self.QMB[h]])
        self.proj_fm("w_in", l, C_Q * 128, 8, self.hchunks(), h_q)

    def build_dg3(self, l):
        for c in range(4):
            self.tt("dve", self.DG3[:, c], self.ident_bf[:].unsqueeze(1).broadcast_to([128, 3, 128]),
                    self.wsc[:, l, c, :].unsqueeze(2).broadcast_to([128, 3, 128]), ALU.mult,
                    [self.CONSTB, self.VECB], [self.DG3B])

    def conv_a(self, l, segs):
        for c in range(4):
            pi, ps, pb = self.ps_get()
            for (so, do, n) in segs:
                self.mm(ps[:, do:do + n], [(self.DG3[:, c, j, :], self.SW[:, c, so + j:so + j + n]) for j in range(3)],
                        [self.DG3B, self.SWB[c]], [pb])
            self.tt("dve", self.AIN[:, c, :], ps[:], self.BG[:, c, :], ALU.mult, [pb, self.BGB[c]], [self.AINB[c]])
            self.ps_put(pi)

    def conv_c(self, l, segs):
        p1i, ps1, pb1 = self.ps_get()
        p2i, ps2, pb2 = self.ps_get()
        for c in range(4):
            d = 0
            self.tt("dve", self.DG31[:, d], self.ident_bf[:].unsqueeze(1).broadcast_to([128, 31, 128]),
                    self.wcf_l[l][:, c, :].unsqueeze(2).broadcast_to([128, 31, 128]), ALU.mult,
                    [self.CONSTB, self.VECB], [self.DG31B[d]])
            pi, ps, pb = self.ps_get()
            for (so, do, n) in segs:
                self.mm(ps[:, do:do + n],
                        [(self.DG31[:, d, j, :], self.UW[:, c, so + j:so + j + n]) for j in range(31)],
                        [self.DG31B[d], self.UWB[c]], [pb])
            bcol = self.v5["cf_conv_b"][:, l, c:c + 1]
            self.act(self.XB[:, c, :], ps[:], AF.Identity, [pb, self.VECB], [self.XBB[c]], bias=bcol)
            zb, zbb = self.sq()
            self.act(zb, ps[:], AF.Identity, [pb, self.VECB], [zbb], bias=bcol)
            self.mm(ps1[:], [(self.ones_bf[:], zb)], [zbb, self.CONSTB], [pb1], start=(c == 0), stop=(c == 3))
            self.ps_put(pi)
            self.stats_add(ps2[:], pb2, self.XB[:, c, :], [self.XBB[c]], c, 4)
        mean, meanb = self.tmp()
        msq, msqb = self.tmp()
        var, varb = self.tmp()
        self.ts("dve", mean, ps1[:], 1.0 / 512, ALU.mult, [pb1], [meanb])
        self.tt("dve", msq, mean, mean, ALU.mult, [meanb], [msqb])
        self.stt("dve", var, ps2[:], 1.0 / 512, msq, ALU.mult, ALU.subtract, [pb2, msqb], [varb])
        self.ps_put(p1i); self.ps_put(p2i)
        sd, sdb = self.tmp()
        self.act(sd, var, AF.Ln, [varb, self.CONSTB], [sdb], bias=self.eps_col[:], scale=1.0)
        self.act(self.RSTD[:], sd, AF.Exp, [sdb], [self.RSTDB], scale=-0.5)
        for c in range(4):
            t1, t1b = self.tmp()
            self.tt("dve", t1, self.XB[:, c, :], mean, ALU.subtract, [self.XBB[c], meanb], [t1b])
            self.tt("dve", t1, t1, self.RSTD[:], ALU.mult, [t1b, self.RSTDB], [t1b])
            self.act(self.ZS[:, c, :], t1, AF.Silu, [t1b, self.VECB], [self.ZSB[c]],
                     bias=self.v5["cf_ln_b"][:, l, c:c + 1], scale=self.v5["cf_ln_g"][:, l, c:c + 1])

    def attention_main(self, l, h, q0, nq, chunks, o_out, o_bufs):
        accs = [self.ps_get() for _ in range(3)]
        n = len(chunks)
        sbanks = {}
        DEPTH = 2
        zs = self.z_i; self.z_i = 1 - zs
        ZB = self.ZBS[zs]

        def qk(j):
            kT, kb, _, _ = chunks[j]
            a = self.ps_get(); b = self.ps_get()
            sbanks[j] = (a, b)
            self.mm(a[1][:, :nq], [(kT[0:64, :], self.QM[0:64, h, q0:q0 + nq])], kb + [self.QMB[h]], [a[2]],
                    tile_pos=(0, 0))
            self.mm(b[1][:, :nq], [(kT[64:128, :], self.QM[64:128, h, q0:q0 + nq])], kb + [self.QMB[h]], [b[2]],
                    tile_pos=(64, 0))

        def pv(j):
            _, _, v, vb = chunks[j]
            a, b = sbanks.pop(j)
            e0 = 2 * (j % 3)
            for m, s in enumerate((a, b)):
                self.act(self.E[:, e0 + m, :nq], s[1][:, :nq], AF.Exp, [s[2]], [self.EB[e0 + m]], scale=0.125)
                self.ps_put(s[0])
            zp = self.ZACC[:, zs, 0, :nq]
            if j == 0:
                self.copy("dve", zp, self.E[:, e0, :nq], [self.EB[e0]], [ZB])
            else:
                self.tt("dve", zp, zp, self.E[:, e0, :nq], ALU.add, [self.EB[e0], ZB], [ZB])
            for m in range(2):
                self.mm(accs[m][1][:, :nq], [(v, self.E[:, e0 + m, :nq])], vb + [self.EB[e0 + m]], [accs[m][2]],
                        start=(j == 0), stop=(j == n - 1))
            self.mm(accs[2][1][:, :nq], [(self.ones_bf[:], self.E[:, e0 + 1, :nq])], [self.CONSTB, self.EB[e0 + 1]],
                    [accs[2][2]], start=(j == 0), stop=(j == n - 1))
        for j in range(min(DEPTH, n)):
            qk(j)
        for j in range(n):
            pv(j)
            if j + DEPTH < n:
                qk(j + DEPTH)
        o0, o0b = self.tmp(); o1, o1b = self.tmp()
        self.copy("dve", o0[:, :nq], accs[0][1][:, :nq], [accs[0][2]], [o0b])
        self.copy("dve", o1[:, :nq], accs[1][1][:, :nq], [accs[1][2]], [o1b])
        self.copy("dve", self.ZACC[:, zs, 1, :nq], accs[2][1][:, :nq], [accs[2][2]], [ZB])
        for a in accs:
            self.ps_put(a[0])
        return dict(l=l, nq=nq, zs=zs, o0=o0, o0b=o0b, o1=o1, o1b=o1b, o_out=o_out, o_bufs=o_bufs)

    def attention_fin(self, st):
        if st is None:
            return
        l, nq, zs = st["l"], st["nq"], st["zs"]
        o0, o0b, o1, o1b = st["o0"], st["o0b"], st["o1"], st["o1b"]
        ZB = self.ZBS[zs]
        V = [self.VECB]
        r0, r0b = self.tmp(); r1, r1b = self.tmp()
        zi, zps, zpb = self.ps_get()
        self.mm(zps[:, :nq], [(self.ones_f[:], self.ZACC[:, zs, 0, :nq])], [ZB, self.CONSTB], [zpb])
        self.act(r0[:, :nq], zps[:, :nq], AF.Ln, [zpb], [r0b])
        self.ps_put(zi)
        self.act(r0[:, :nq], r0[:, :nq], AF.Exp, [r0b], [r0b], scale=-1.0)
        self.act(r1[:, :nq], self.ZACC[:, zs, 1, :nq], AF.Ln, [ZB], [r1b])
        self.act(r1[:, :nq], r1[:, :nq], AF.Exp, [r1b], [r1b], scale=-1.0)
        self.tt("dve", o0[:, :nq], o0[:, :nq], r0[:, :nq], ALU.mult, [o0b, r0b], [o0b])
        self.stt("dve", o1[:, :nq], o1[:, :nq], self.lamt[:, 5, l:l + 1], r1[:, :nq], ALU.mult, ALU.mult,
                 [o1b, r1b] + V, [o1b])
        self.tt("dve", o0[:, :nq], o0[:, :nq], o1[:, :nq], ALU.add, [o0b, o1b], [o0b])
        pi, ps, pb = self.ps_get()
        self.stats_add(ps[:, :nq], pb, o0[:, :nq], [o0b], 0, 1)
        self.stats_rstd(ps, pb, 128, nq)
        self.ps_put(pi)
        self.tt("dve", o0[:, :nq], o0[:, :nq], self.RSTD[:, :nq], ALU.mult, [o0b, self.RSTDB], [o0b])
        self.act(st["o_out"], o0[:, :nq], AF.Identity, [o0b] + V, st["o_bufs"], scale=self.lamt[:, 6, l:l + 1])

    def attn_prompt(self, l):
        prev = None
        for bi in range(2):
            for h in range(H):
                chunks = []
                for tb in (2 * bi, 2 * bi + 1):
                    kT = self.R[:, h * T + tb * 128:h * T + (tb + 1) * 128]
                    u0 = 8 + 2 * tb + (h // 4)
                    v = self.R[:, u0 * T + (h % 4) * 128:u0 * T + (h % 4 + 1) * 128]
                    chunks.append((kT, [self.RB[h]], v, [self.RB[u0]]))
                st = self.attention_main(l, h, bi * 256, 256, chunks, self.OT[:, h, bi * 256:(bi + 1) * 256],
                                         [self.OTB[h]])
                self.attention_fin(prev)
                prev = st
        self.attention_fin(prev)

    def load_kv(self, h):
        s = h % 2
        ku = self.RB[9 * s:9 * s + 9]
        vu = self.RB[18 + 9 * s:18 + 9 * s + 9]
        K = self.R[:, s * 4608:(s + 1) * 4608]
        Vv = self.R[:, 9216 + s * 4608:9216 + (s + 1) * 4608].rearrange("p (k v) -> p k v", v=128)
        GK = self.GOUTP[h // 4]
        GV = self.GOUTP[2 + h // 4]
        r0 = (h % 4) * 128

        def fk(e):
            return [e.dma_start(out=K[:, 0:512], in_=self.KCTX[:, h, :]),
                    e.dma_start(out=K[:, 512:2560], in_=GK[r0:r0 + 128, :]),
                    e.dma_start(out=K[:, 2560:4608], in_=GK[512 + r0:512 + r0 + 128, :])]
        self.P.dma("sp", fk, f"kv{s}", [self.KCTXB, self.GOUTB[h // 4]], ku, n=3)

        def fv(e):
            return [e.dma_start(out=Vv[:, 0:4, :], in_=self.VCTX[h]),
                    e.dma_start(out=Vv[:, 4:20, :],
                                in_=GV[r0:r0 + 128, :].rearrange("p (k v) -> p k v", v=128)),
                    e.dma_start(out=Vv[:, 20:36, :],
                                in_=GV[512 + r0:512 + r0 + 128, :].rearrange("p (k v) -> p k v", v=128))]
        self.P.dma("sp", fv, f"kv{2 + s}", [self.VCTXB, self.GOUTB[2 + h // 4]], vu, n=3)
        if self.pending:
            self.issue_piece([vu[0]])

    def attn_sample(self, l):
        self.load_kv(0)
        prev = None
        for h in range(H):
            if h + 1 < H:
                self.load_kv(h + 1)
            s = h % 2
            chunks = []
            for j in range(36):
                kT = self.R[:, s * 4608 + j * 128:s * 4608 + (j + 1) * 128]
                v = self.R[:, 9216 + s * 4608 + j * 128:9216 + s * 4608 + (j + 1) * 128]
                chunks.append((kT, self.RB[9 * s:9 * s + 9], v, self.RB[18 + 9 * s:18 + 9 * s + 9]))
            st = self.attention_main(l, h, 0, T, chunks, self.OT[:, h, :], [self.OTB[h]])
            self.attention_fin(prev)
            prev = st
        self.attention_fin(prev)

    def merge(self, l):
        H_ = self.hchunks()
        ain = [(self.AIN[:, c, :], self.AINB[c]) for c in range(4)]
        ot = [(self.OT[:, c, :], self.OTB[c]) for c in range(8)]
        zs = [(self.ZS[:, c, :], self.ZSB[c]) for c in range(4)]
        for pas, (cg, wname, rhs) in enumerate(((C_GA, "sc_w_out", ain), (C_GB, "attn_w_out", ot),
                                                (C_GC, "cf_w_out", zs))):
            for jg in range(2):
                sg = [self.tmp() for _ in range(4)]

                def h_gate(q, ps, pb, sg=sg):
                    self.act(sg[q][0], ps[:], AF.Sigmoid, [pb], [sg[q][1]])
                self.proj_fm("w_in", l, cg * 128 + jg * T, 4, H_, h_gate)

                def h_y(q, ps, pb, sg=sg, jg=jg, pas=pas):
                    j = jg * 4 + q
                    if pas == 0:
                        self.tt("dve", self.XB[:, j, :], ps[:], sg[q][0], ALU.mult, [pb, sg[q][1]], [self.XBB[j]])
                    elif pas == 1:
                        t, tb = self.tmp()
                        self.tt("dve", t, ps[:], sg[q][0], ALU.mult, [pb, sg[q][1]], [tb])
                        self.tt("dve", self.XB[:, j, :], self.XB[:, j, :], t, ALU.add, [tb, self.XBB[j]],
                                [self.XBB[j]])
                    else:
                        t, tb = self.tmp()
                        self.stt("dve", t, ps[:], self.vD["cf_b_out"][:, l, j:j + 1], sg[q][0], ALU.add, ALU.mult,
                                 [pb, sg[q][1], self.VECB], [tb])
                        self.tt("dve", self.QM[:, j, :], self.XB[:, j, :], t, ALU.add, [tb, self.XBB[j]],
                                [self.QMB[j]])
                self.proj_fm(wname, l, jg * T, 4, rhs, h_y)

    def out_proj_residual(self, l, g, wname, rhs, ig):
        si, sps, spb = self.ps_get()

        def h(j, ps, pb):
            self.copy("dve", self.XB[:, j, :], ps[:], [pb], [self.XBB[j]])
            self.stats_add(sps[:], spb, ps[:], [pb], j, 8)
        self.proj_fm(wname, l, 0, 8, rhs, h)
        self.stats_rstd(sps, spb, D)
        self.ps_put(si)
        self.post_norm_residual(l, g, ig)

    def mlp(self, l, g):
        self.norm_mod(l, g, 3, 4)

        def h_ff1(j, ps, pb):
            t, tb = self.tmp()
            self.act(t, ps[:], AF.Relu, [pb], [tb])
            self.tt("dve", self.R[:, j * T:(j + 1) * T], t, t, ALU.mult, [tb], [self.RB[j]])
        self.proj_fm("w_ff1", l, 0, 32, self.hchunks(), h_ff1)
        f = [(self.R[:, j * T:(j + 1) * T], self.RB[j]) for j in range(32)]
        self.out_proj_residual(l, g, "w_ff2", f, 5)

    def back_half(self, l, g, kind, ti):
        self.merge(l)
        m = [(self.QM[:, j, :], self.QMB[j]) for j in range(8)]
        self.out_proj_residual(l, g, "w_o", m, 2)
        self.mlp(l, g)
        self.store_x(kind, ti, l)

    def prompt_tile(self, l, ti):
        stage = self.stage
        dbg = (l == 0 and ti == 0)
        self.load_x("p", ti, l)
        self.norm_mod(l, 0, 0, 1)
        if dbg:
            self.dump("coef", self.coef[:], self.COEFB)
            self.dump("H", self.Hh[:], self.HB)
        if stage == 3:
            return
        self.front_prompt(l, ti)
        if dbg:
            self.dump("QM", self.QM[:], self.QMB)
            self.dump("SW", self.SW[:], self.SWB)
            self.dump("UW", self.UW[:], self.UWB)
        if stage == 4:
            return
        self.conv_a(l, [(0, 0, 256), (258, 256, 256)])
        self.conv_c(l, [(0, 0, 256), (286, 256, 256)])
        if dbg:
            self.dump("AIN", self.AIN[:], self.AINB)
            self.dump("ZS", self.ZS[:], self.ZSB)
        if stage == 5:
            return
        self.attn_prompt(l)
        if dbg:
            self.dump("OT", self.OT[:], self.OTB)
        if stage == 6:
            return
        self.merge(l)
        if dbg:
            self.dump("M", self.QM[:], self.QMB)
        m = [(self.QM[:, j, :], self.QMB[j]) for j in range(8)]
        self.out_proj_residual(l, 0, "w_o", m, 2)
        if dbg:
            self.dump("XMID", self.XA[:], self.XAB)
        if stage == 7:
            return
        self.mlp(l, 0)
        self.store_x("p", ti, l)
        self.swap_x()

    def swap_x(self):
        self.XA, self.XB = self.XB, self.XA
        self.XAB, self.XBB = self.XBB, self.XAB

    def sample_a(self, l, ti):
        self.load_x("s", ti, l)
        self.norm_mod(l, 1, 0, 1)
        self.front_sample_a(l, ti)
        self.swap_x()

    def sample_b(self, l, ti):
        self.load_x("s", ti, l)
        self.norm_mod(l, 1, 0, 1)
        self.front_sample_b(l, ti)
        self.conv_a(l, [(0, 0, T)])
        self.conv_c(l, [(0, 0, T)])
        self.attn_sample(l)
        self.back_half(l, 1, "s", ti)
        self.swap_x()

    def convert_ctx(self, l):
        for kb in range(4):
            xi = self.xt_i; self.xt_i = 1 - xi
            self.dma("sp", self.XT[:, xi, :], self.I["ck"][l, kb * 128:(kb + 1) * 128, :], f"xt{xi}", [],
                     [self.XTB[xi]])
            for g in range(2):
                pi, ps, pb = self.ps_get()

                def fn(e, xi=xi, g=g, ps=ps):
                    r = None
                    for q in range(4):
                        h = g * 4 + q
                        r = e.transpose(out=ps[:, q * 128:(q + 1) * 128], in_=self.XT[:, xi, h * 128:(h + 1) * 128],
                                        identity=self.ident[:])
                    return r
                self.P.op("pe", fn, [self.XTB[xi], self.CONSTB], [pb])
                self.copy("act" if g else "dve", self.OT[:, g * 4:(g + 1) * 4, kb * 128:(kb + 1) * 128],
                          ps[:].rearrange("p (q t) -> p q t", q=4), [pb], self.OTB[g * 4:(g + 1) * 4])
                self.ps_put(pi)
        self.dma("act", self.KCTX, self.OT[:], "kctx", self.OTB, [self.KCTXB])

        def fv(e):
            return [e.dma_start(out=self.VCTX[h],
                                in_=self.I["cv"][l, :, h * 128:(h + 1) * 128].rearrange("(k p) v -> p k v", p=128))
                    for h in range(H)]
        self.P.dma("pool", fv, "vctx", [], [self.VCTXB], n=H)

    def exchange(self, l):
        rg = [[2 * i, 2 * i + 1] for i in range(self.ncores // 2)]
        for i in range(5):
            def fn(e, i=i):
                return [e.collective_compute("AllGather", ALU.bypass, replica_groups=rg,
                                             ins=[self.GINP[i]], outs=[self.GOUTP[i]])]
            self.P.dma("pool", fn, f"cc{i}", [self.GINB[i]], [self.GOUTB[i]], inc=1)

    def halo_fix(self, l):
        G = self.GOUTP[4]
        h0 = G[0:8, :].rearrange("r (q e) -> (r q) e", e=128)
        h1 = G[8:16, :].rearrange("r (q e) -> (r q) e", e=128)
        hal = self.hal
        self.P.dma("sp", lambda e: [e.dma_start(out=hal[:, 0, :], in_=h0), e.dma_start(out=hal[:, 1, :], in_=h1)],
                   "hal", [self.GOUTB[4]], [self.HALB], n=2)
        self.ts("dve", hal[:, 0, :], hal[:, 0, :], self.masks[:, 0:1], ALU.mult, [self.HALB, self.CONSTB], [self.HALB])
        self.ts("dve", hal[:, 1, :], hal[:, 1, :], self.masks[:, 1:2], ALU.mult, [self.HALB, self.CONSTB], [self.HALB])

        def fw(e):
            return [
                e.dma_start(out=self.SUU[:, :, 0:15], in_=hal[:, 0, 60:120].rearrange("p (c k) -> p c k", k=15), allow_slow_non_contiguous=True),
                e.dma_start(out=self.SUS[:, :, 0:1], in_=hal[:, 0, 124:128].rearrange("p (c k) -> p c k", k=1), allow_slow_non_contiguous=True),
                e.dma_start(out=self.SUU[:, :, 2063:2078], in_=hal[:, 1, 0:60].rearrange("p (c k) -> p c k", k=15), allow_slow_non_contiguous=True),
                e.dma_start(out=self.SUS[:, :, 2049:2050], in_=hal[:, 1, 120:124].rearrange("p (c k) -> p c k", k=1), allow_slow_non_contiguous=True),
            ]
        self.P.dma("act", fw, "halw", [self.HALB], [self.SUUB, self.SUSB], n=4)


    def build_body(self):
        stage = self.stage
        self.setup()
        if stage == 0:
            return
        rest = ["sc_w_out", "attn_w_out", "cf_w_out", "w_o", "w_ff1", "w_ff2"]
        self.queue_conversions(0, ["w_mod", "w_in"] + rest)
        two = self.nlayers > 1
        if two:
            self.queue_conversions(1, ["w_mod", "w_in"] + rest)
        self.flush_until(("w_in_a", 0))
        self.compute_mod(0)
        if stage <= 2:
            return
        for l in range(self.nlayers):
            self.build_dg3(l)
            if self.do_sample:
                self.convert_ctx(l)
                for ti in range(NST):
                    self.sample_a(l, ti)
                self.exchange(l)
            if self.do_prompt:
                for ti in range(NPT if stage >= 99 else 1):
                    self.prompt_tile(l, ti)
                    if ti == 0:
                        if l == 0 and two:
                            self.compute_mod(1)
                        if self.do_sample:
                            self.halo_fix(l)
            else:
                if l == 0 and two:
                    self.compute_mod(1)
                if self.do_sample:
                    self.halo_fix(l)
            if self.do_sample:
                for ti in range(NST):
                    self.sample_b(l, ti)
        while self.pending:
            self.issue_piece()

    def build(self):
        with ExitStack() as st:
            self.st = st
            self.P = Prog(self.nc, st)
            self.P.trace_ops = self.trace_ops
            self.alloc()
            for i in range(self.NTMP):
                self.P.newsem(f"st_tmp{i}")
            self.build_body()
            if self.DBG:
                self.out_toks.append(("dbg", self.P.cnt["dbg"]))
            self.P.wait_all("sp", self.out_toks + [(k, v) for k, v in self.P.cnt.items() if v > 0 and k != 'E_sp'])
            self.P.run_block()
        return self.nc


def _rope_tables(rank):
    t = np.arange(2048, dtype=np.int64) + rank * 2048
    row = (t // 64).astype(np.float32)
    col = (t % 64).astype(np.float32)
    inv = (np.float32(10000.0) ** (-np.arange(16, dtype=np.float32) / np.float32(16))).astype(np.float32)
    C = np.zeros((128, 2048), np.float32)
    S = np.zeros((128, 2048), np.float32)
    for p in range(128):
        d = p % 64
        pos = row if d < 32 else col
        dd = d % 32
        ang = (pos * inv[dd % 16]).astype(np.float32)
        C[p] = np.cos(ang)
        S[p] = -np.sin(ang) if dd < 16 else np.sin(ang)
    return C, S


def _perm():
    Pm = np.zeros((128, 128), np.float32)
    for m in range(128):
        dd = (m % 64) % 32
        partner = m + 16 if dd < 16 else m - 16
        Pm[partner, m] = 1.0
    return Pm


_CACHE = {}


def make_in_maps(inputs):
    f = lambda a: np.ascontiguousarray(np.asarray(a, dtype=np.float32))
    shared = {k: f(inputs[k]) for k in IN_SHAPES if k in inputs}
    ident = np.eye(128, dtype=np.float32)
    perm = _perm()
    xp = f(inputs["x_prompt"]); xs = f(inputs["x_sample"])
    ck = f(inputs["cache_k"]); cv = f(inputs["cache_v"])
    c = f(inputs["c"]); c_ctx = f(inputs["c_ctx"])
    maps = []
    for core in range(8):
        b, r = core // 2, core % 2
        C, S = _rope_tables(r)
        m = dict(shared)
        m["xp"] = np.ascontiguousarray(xp[4 * core:4 * core + 4].reshape(1024, D))
        m["xs"] = np.ascontiguousarray(xs[b, r * 2048:(r + 1) * 2048])
        m["ck"] = np.ascontiguousarray(ck[b].reshape(L, 512, D))
        m["cv"] = np.ascontiguousarray(cv[b].reshape(L, 512, D))
        m["cvec"] = np.ascontiguousarray(np.stack([c_ctx, c[b]], 0))
        m["rope_c"] = C; m["rope_s"] = S; m["perm"] = perm; m["ident"] = ident
        mk = np.zeros((128, 2), np.float32)
        mk[:, 0] = 1.0 if r == 1 else 0.0
        mk[:, 1] = 1.0 if r == 0 else 0.0
        m["masks"] = mk
        maps.append(m)
    return maps


def kernel(**inputs):
    if "nc" not in _CACHE:
        _CACHE["nc"] = Builder().build()
    nc = _CACHE["nc"]
    maps = make_in_maps(inputs)
    res = run_bass_kernel_spmd(nc, maps, core_ids=list(range(8)))
    R = res.results
    y_p = np.concatenate([R[c]["yp"].reshape(4, 256, D) for c in range(8)], 0)
    y_s = np.stack([np.concatenate([R[2 * b]["ys"], R[2 * b + 1]["ys"]], 0) for b in range(4)], 0)
    nk = np.concatenate([R[c]["nk"] for c in range(8)], 0).reshape(32, L, 256, H, 2, 64)
    nv = np.concatenate([R[c]["nv"] for c in range(8)], 0).reshape(32, L, 256, H, 128)
    return (y_p.astype(np.float32), y_s.astype(np.float32), nk.astype(np.float32), nv.astype(np.float32))
```

```python
import math
from collections import deque
from contextlib import ExitStack

import numpy as np
import concourse.bass as bass
import concourse.mybir as mybir
from concourse.bass_utils import run_bass_kernel_spmd

F32 = mybir.dt.float32
BF16 = mybir.dt.bfloat16
AF = mybir.ActivationFunctionType
ALU = mybir.AluOpType

D = 1024
T = 512
L = 2
H = 8
NCOL = 8704
DFF = 4096
NPT = 2
NST = 4
EPS = 1e-6
C_BG, C_CG, C_XIN, C_Q, C_K, C_V, C_ZA, C_ZG, C_GA, C_GB, C_GC = 0, 4, 8, 12, 20, 28, 36, 40, 44, 52, 60
GROWS = 2056


class Buf:
    __slots__ = ("name", "w", "r", "excl")

    def __init__(self, name="", excl=False):
        self.name = name
        self.w = None
        self.r = []
        self.excl = excl


def bufs(name, n, excl=False):
    return [Buf(f"{name}{i}", excl) for i in range(n)]


class Prog:
    ENGS = ("pe", "act", "dve", "pool", "sp")

    def __init__(self, nc, stack):
        self.nc = nc
        self.stack = stack
        self.ops = {e: [] for e in self.ENGS}
        self.sems = {}
        self.cnt = {}
        self.seen = {e: {} for e in self.ENGS}
        self.trace_ops = False
        for e in self.ENGS:
            self.newsem("E_" + e)

    def newsem(self, key):
        self.sems[key] = self.stack.enter_context(self.nc.semaphore(key))
        self.cnt[key] = 0
        return key

    def _waits(self, eng, reads, writes):
        need = {}

        def add(tok):
            if tok is None:
                return
            k, v = tok
            if need.get(k, 0) < v:
                need[k] = v
        for b in reads:
            add(b.w)
            if b.excl:
                for t in b.r:
                    add(t)
        for b in writes:
            add(b.w)
            for t in b.r:
                add(t)
        out = []
        seen = self.seen[eng]
        for k, v in need.items():
            if k == "E_pe" and eng == "pe":
                continue
            if not k.startswith("E_"):
                v = self.cnt[k]
            if seen.get(k, 0) < v:
                seen[k] = v
                out.append((k, v))
        return out

    def _commit(self, tok, reads, writes):
        for b in reads:
            b.r.append(tok)
        for b in writes:
            b.w = tok
            b.r = []

    def op(self, eng, fn, reads=(), writes=()):
        waits = self._waits(eng, reads, writes)
        key = "E_" + eng
        self.cnt[key] += 1
        tok = (key, self.cnt[key])
        self.ops[eng].append((waits, fn, key, 1))
        self._commit(tok, reads, writes)
        if self.trace_ops:
            import traceback
            fn._where = [f"{f.name}:{f.lineno}" for f in traceback.extract_stack(limit=6)[:-1]]
        return tok

    def dma(self, eng, fn, semkey, reads=(), writes=(), n=1, inc=16, after=()):
        waits = self._waits(eng, list(reads) + list(after), writes)
        self.cnt[semkey] += inc * n
        tok = (semkey, self.cnt[semkey])
        self.ops[eng].append((waits, fn, semkey, inc))
        self._commit(tok, reads, writes)
        if self.trace_ops:
            import traceback
            fn._where = [f"{f.name}:{f.lineno}" for f in traceback.extract_stack(limit=6)[:-1]]
        return tok

    def wait_all(self, eng, toks):
        waits = []
        for k, v in toks:
            if self.seen[eng].get(k, 0) < v:
                self.seen[eng][k] = v
                waits.append((k, v))
        self.ops[eng].append((waits, None, None, 0))

    def emit(self, eng, e):
        sems = self.sems
        for waits, fn, key, inc in self.ops[eng]:
            for k, v in waits:
                e.wait_ge(sems[k], v)
            if fn is None:
                continue
            try:
                r = fn(e)
            except Exception:
                print("FAILED OP at", getattr(fn, "_where", None))
                raise
            if isinstance(r, (list, tuple)):
                for i in r:
                    if inc == 1:
                        i.then_inc(sems[key])
                    else:
                        i.then_inc(sems[key], inc)
            else:
                r.then_inc(sems[key], inc)

    def run_block(self):
        nc = self.nc
        with nc.Block() as block:
            @block.tensor
            def _(e):
                self.emit("pe", e)

            @block.scalar
            def _(e):
                self.emit("act", e)

            @block.vector
            def _(e):
                self.emit("dve", e)

            @block.gpsimd
            def _(e):
                self.emit("pool", e)

            @block.sync
            def _(e):
                self.emit("sp", e)


IN_SHAPES = {
    "xp": [1024, D], "xs": [2048, D], "ck": [L, 512, D], "cv": [L, 512, D], "cvec": [2, D],
    "w_mod": [L, D, 6 * D], "b_mod": [L, 6 * D],
    "g_pre_mix": [L, D], "g_post_mix": [L, D], "g_pre_mlp": [L, D], "g_post_mlp": [L, D],
    "w_in": [L, D, NCOL], "sc_conv_w": [L, 3, 512], "sc_w_out": [L, 512, D],
    "lam_q1": [L, 64], "lam_k1": [L, 64], "lam_q2": [L, 64], "lam_k2": [L, 64],
    "subln_g": [L, 128], "attn_w_out": [L, D, D], "cf_conv_w": [L, 31, 512],
    "cf_conv_b": [L, 512], "cf_ln_g": [L, 512], "cf_ln_b": [L, 512],
    "cf_w_out": [L, 512, D], "cf_b_out": [L, D], "w_o": [L, D, D],
    "w_ff1": [L, D, DFF], "w_ff2": [L, DFF, D],
    "rope_c": [128, 2048], "rope_s": [128, 2048], "perm": [128, 128], "ident": [128, 128],
    "masks": [128, 2],
}
OUT_SHAPES = {"yp": [1024, D], "ys": [2048, D], "nk": [4, L, 256, D], "nv": [4, L, 256, D]}
BIGW = {"w_in": (D, NCOL), "sc_w_out": (512, D), "attn_w_out": (D, D), "cf_w_out": (512, D),
        "w_o": (D, D), "w_ff1": (D, DFF), "w_ff2": (DFF, D)}


class Builder:
    def __init__(self, nlayers=L, do_sample=True, do_prompt=True, dbg=()):
        self.nlayers = nlayers
        self.do_sample = do_sample
        self.do_prompt = do_prompt
        self.dbg = dict(dbg)
        self.trace_ops = False
        self.stage = 99
        self.ncores = 8
        self.sub = 0
        self.nc = bass.Bass("TRN2", target_bir_lowering=False)
        nc = self.nc
        self.I = {k: nc.dram_tensor(k, s, F32, kind="ExternalInput").ap() for k, s in IN_SHAPES.items()}
        self.O = {k: nc.dram_tensor(k, s, F32, kind="ExternalOutput").ap() for k, s in OUT_SHAPES.items()}
        self.DBG = {k: nc.dram_tensor("dbg_" + k, s, F32, kind="ExternalOutput").ap() for k, s in self.dbg.items()}
        self.W16 = {k: nc.dram_tensor(k + "_16", [L, r, c], BF16).ap() for k, (r, c) in BIGW.items()}
        self.XS = nc.dram_tensor("xs_scr", [NPT + NST, 128, 8, T], F32).ap()
        self.GINP = [nc.dram_tensor(f"gin{i}", [512 if i < 4 else 8, 2048], BF16).ap() for i in range(5)]
        self.GOUTP = [nc.dram_tensor(f"gout{i}", [1024 if i < 4 else 16, 2048], BF16).ap() for i in range(5)]
        self.KCTX = nc.dram_tensor("kctx", [128, H, 512], BF16).ap()
        self.VCTX = nc.dram_tensor("vctx", [H, 128, 4, 128], BF16).ap()
        self.SUS = nc.dram_tensor("sus", [128, 4, 2050], BF16).ap()
        self.SUU = nc.dram_tensor("suu", [128, 4, 2078], BF16).ap()
        self.MODS = nc.dram_tensor("mods", [L, 2, 6 * D], F32).ap()
        self.WM16 = nc.dram_tensor("wm16", [L, D, 6 * D], BF16).ap()

    def sb(self, name, shape, dt):
        return self.st.enter_context(self.nc.sbuf_tensor("s_" + name, list(shape), dt))

    def act(self, out, in_, func, reads, writes, bias=None, scale=None):
        kw = {}
        if bias is not None:
            kw["bias"] = bias
        if scale is not None:
            kw["scale"] = scale
        return self.P.op("act", lambda e: e.activation(out=out, in_=in_, func=func, **kw), reads, writes)

    def tt(self, eng, out, in0, in1, op, reads, writes):
        return self.P.op(eng, lambda e: e.tensor_tensor(out=out, in0=in0, in1=in1, op=op), reads, writes)

    def ts(self, eng, out, in0, s1, op0, reads, writes, s2=None, op1=None):
        if s2 is None:
            return self.P.op(eng, lambda e: e.tensor_scalar(out=out, in0=in0, scalar1=s1, scalar2=None, op0=op0),
                             reads, writes)
        return self.P.op(eng, lambda e: e.tensor_scalar(out=out, in0=in0, scalar1=s1, scalar2=s2, op0=op0, op1=op1),
                         reads, writes)

    def stt(self, eng, out, in0, scalar, in1, op0, op1, reads, writes):
        return self.P.op(eng, lambda e: e.scalar_tensor_tensor(out=out, in0=in0, scalar=scalar, in1=in1,
                                                               op0=op0, op1=op1), reads, writes)

    def copy(self, eng, out, in_, reads, writes):
        if eng == "act":
            return self.P.op("act", lambda e: e.copy(out=out, in_=in_), reads, writes)
        return self.P.op(eng, lambda e: e.tensor_copy(out=out, in_=in_), reads, writes)

    def mm(self, ps_ap, pairs, reads, writes, start=True, stop=True, tile_pos=None):
        pairs = list(pairs)
        n = len(pairs)

        def fn(e):
            r = None
            for i, (lhsT, rhs) in enumerate(pairs):
                kw = {}
                if tile_pos is not None:
                    kw["tile_position"] = tile_pos
                r = e.matmul(ps_ap, lhsT=lhsT, rhs=rhs, start=(start and i == 0), stop=(stop and i == n - 1), **kw)
            return r
        return self.P.op("pe", fn, reads, writes)

    def dma(self, eng, out, in_, sem, reads, writes, slow=False):
        if slow:
            return self.P.dma(eng, lambda e: [e.dma_start(out=out, in_=in_, allow_slow_non_contiguous=True)], sem,
                              reads, writes)
        return self.P.dma(eng, lambda e: [e.dma_start(out=out, in_=in_)], sem, reads, writes)

    def ps_get(self):
        i = self.ps_free.popleft()
        return i, self.PS[i], self.PSB[i]

    def ps_put(self, i):
        self.ps_free.append(i)

    def tmp(self):
        i = self.tmp_i
        self.tmp_i = (i + 1) % self.NTMP
        return self.TMP[:, i, :], self.TMPB[i]

    def sq(self):
        i = self.sq_i
        self.sq_i = (i + 1) % 2
        return self.SQ[:, i, :], self.SQB[i]

    def dump(self, name, src_ap, src_bufs):
        if name in self.DBG and not self.dumped.get(name):
            self.dumped[name] = True
            self.dma("pool", self.DBG[name], src_ap, "dbg", src_bufs, [])

    def alloc(self):
        nc, sb = self.nc, self.sb
        P = self.P
        self.XA = sb("XA", [128, 8, T], F32); self.XAB = bufs("XA", 8)
        self.XB = sb("XB", [128, 8, T], F32); self.XBB = bufs("XB", 8)
        self.Hh = sb("Hh", [128, 8, T], BF16); self.HB = bufs("H", 8)
        self.NWR = 3
        self.WR = sb("WR", [128, self.NWR, 8, T], BF16); self.WRB = bufs("WR", self.NWR)
        self.wr_i = 0
        self.R = sb("R", [128, 36 * 512], BF16); self.RB = bufs("R", 36)
        self.QM = sb("QM", [128, 8, T], BF16); self.QMB = bufs("QM", 8)
        self.BG = sb("BG", [128, 4, T], BF16); self.BGB = bufs("BG", 4)
        self.SW = sb("SW", [128, 4, 516], BF16); self.SWB = bufs("SW", 4)
        self.UW = sb("UW", [128, 4, 572], BF16); self.UWB = bufs("UW", 4)
        self.AIN = sb("AIN", [128, 4, T], BF16); self.AINB = bufs("AIN", 4)
        self.ZS = sb("ZS", [128, 4, T], BF16); self.ZSB = bufs("ZS", 4)
        self.OT = sb("OT", [128, 8, T], BF16); self.OTB = bufs("OT", 8)
        self.E = sb("E", [128, 6, T], BF16); self.EB = bufs("E", 6)
        self.ZACC = sb("ZACC", [128, 2, 2, T], F32); self.ZBS = bufs("ZACC", 2); self.z_i = 0
        self.ones_f = sb("ones_f", [128, 128], F32)
        self.ROPE = sb("ROPE", [128, 2, T], F32); self.ROPEB = Buf("ROPE")
        self.DG3 = sb("DG3", [128, 4, 3, 128], BF16); self.DG3B = Buf("DG3")
        self.DG31 = sb("DG31", [128, 1, 31, 128], BF16); self.DG31B = bufs("DG31", 1)
        self.NTMP = 10
        self.TMP = sb("TMP", [128, self.NTMP, T], F32); self.TMPB = bufs("TMP", self.NTMP); self.tmp_i = 0
        self.SQ = sb("SQ", [128, 2, T], BF16); self.SQB = bufs("SQ", 2); self.sq_i = 0
        self.RSTD = sb("RSTD", [128, T], F32); self.RSTDB = Buf("RSTD")
        self.XT = sb("XT", [128, 2, D], F32); self.XTB = bufs("XT", 2); self.xt_i = 0
        self.STG = self.XT; self.STGB = self.XTB; self.stg_i = 0
        self.ident = sb("ident", [128, 128], F32)
        self.ident_bf = sb("ident_bf", [128, 128], BF16)
        self.ones_bf = sb("ones_bf", [128, 128], BF16)
        self.perm_bf = sb("perm_bf", [128, 128], BF16)
        self.eps_col = sb("eps_col", [128, 1], F32)
        self.masks = sb("masks", [128, 2], F32)
        self.CONSTB = Buf("CONST")
        self.VCOL = sb("VCOL", [128, 4, 128], F32)
        A = self.VCOL[:, 0, :]
        self.vD = {k: A[:, 16 * i:16 * i + 16].rearrange("p (l j) -> p l j", l=L) for i, k in
                   enumerate(("g_pre_mix", "g_post_mix", "g_pre_mlp", "g_post_mlp", "cf_b_out"))}
        self.v5 = {k: A[:, 80 + 8 * i:80 + 8 * i + 8].rearrange("p (l j) -> p l j", l=L) for i, k in
                   enumerate(("cf_conv_b", "cf_ln_g", "cf_ln_b"))}
        self.wsc = A[:, 104:128].rearrange("p (l j k) -> p l j k", l=L, j=4)
        self.wcf_l = [self.VCOL[:, 1 + l, 0:124].rearrange("p (j k) -> p j k", j=4) for l in range(L)]
        self.sublng = self.VCOL[:, 3, 16:18]
        self.cvT = self.VCOL[:, 3, 0:16].rearrange("p (g j) -> p j g", g=2)
        self.lamv = sb("lamv", [128, 4, L, 64], F32)
        self.lamt = sb("lamt", [128, 8, L], F32)
        self.VECB = Buf("VEC")
        self.scv = sb("scv", [128, 8, 2], BF16)
        self.modT = sb("modT", [128, L, 2, 48], F32); self.MODTB = bufs("modT", L)
        self.coef = sb("coef", [128, L, 2, 6, 8], F32); self.COEFB = bufs("coef", L)
        self.bmod2 = sb("bmod2", [2, T], F32); self.BMODB = Buf("bmod2")
        self.hal = sb("hal", [128, 2, 128], BF16); self.HALB = Buf("hal")
        self.PS = []
        self.PSB = bufs("PS", 8, excl=True)
        for i in range(8):
            self.PS.append(self.st.enter_context(nc.psum_tensor(f"ps{i}", [128, T], F32)))
        self.ps_free = deque(range(8))
        for k in (["vec", "cst", "cvm0", "cvm1", "conv", "xl", "xt0", "xt1", "stg0", "stg1", "st_xa", "st_qm", "st_ot", "st_ain", "st_zs",
                   "st_misc", "rope", "cc0", "cc1", "cc2", "cc3", "cc4", "mods", "modt", "hal", "halw", "kctx", "vctx", "dbg", "sw", "uw",
                   "st_e", "bmod"]
                  + [f"wr{i}" for i in range(self.NWR)] + [f"kv{i}" for i in range(4)]
                  + [f"cv_{k}{l}" for k in list(BIGW) + ["w_in_a", "w_in_b"] for l in range(L)] + [f"wm{l}" for l in range(L)]):
            P.newsem(k)
        self.W16B = {(k, l): Buf(f"W16{k}{l}") for k in list(BIGW) + ["w_in_a", "w_in_b"] for l in range(L)}
        self.XSB = bufs("XS", NPT + NST)
        self.GINB = bufs("GIN", 5); self.GOUTB = bufs("GOUT", 5)
        self.KCTXB = Buf("KCTX"); self.VCTXB = Buf("VCTX")
        self.SUSB = Buf("SUS"); self.SUUB = Buf("SUU")
        self.MODSB = bufs("MODS", L)
        self.WM16B = bufs("WM16", L)
        self.dumped = {}
        self.out_toks = []
        self.pending = deque()
        self.pace_i = 0

    def setup(self):
        P, I = self.P, self.I
        ident, ident_bf, ones_bf, perm_bf, eps_col = self.ident, self.ident_bf, self.ones_bf, self.perm_bf, self.eps_col
        C = [self.CONSTB]
        self.dma("sp", ident[:], I["ident"], "cst", [], C)
        self.dma("sp", self.masks[:], I["masks"], "cst", [], C)
        self.dma("pool", perm_bf[:], I["perm"], "conv", [], C)
        self.copy("dve", ident_bf[:], ident[:], C, C)
        P.op("dve", lambda e: e.memset(ones_bf[:], 1.0), [], C)
        P.op("dve", lambda e: e.memset(self.ones_f[:], 1.0), [], C)
        P.op("dve", lambda e: e.memset(eps_col[:], EPS), [], C)
        V = [self.VECB]
        RW = self.XT[:, 0, 0:512].rearrange("p (a c) -> p a c", a=4)
        RWB = [self.XTB[0]]
        P.op("dve", lambda e: e.memset(self.XT[:, 0, 0:512], 0.0), [], RWB)
        row = lambda a, r0, n: RW[r0:r0 + n, a, :]
        for i, k in enumerate(("g_pre_mix", "g_post_mix", "g_pre_mlp", "g_post_mlp", "cf_b_out")):
            self.dma("sp", row(0, 16 * i, 16), I[k].rearrange("l (j p) -> (l j) p", p=128), "xt0", [], RWB)
        for i, k in enumerate(("cf_conv_b", "cf_ln_g", "cf_ln_b")):
            self.dma("sp", row(0, 80 + 8 * i, 8), I[k].rearrange("l (j p) -> (l j) p", p=128), "xt0", [], RWB)
        for l in range(L):
            for j in range(4):
                self.dma("sp", row(0, 104 + (l * 4 + j) * 3, 3), I["sc_conv_w"][l, :, j * 128:(j + 1) * 128], "xt0", [], RWB)
                self.dma("sp", row(1 + l, j * 31, 31), I["cf_conv_w"][l, :, j * 128:(j + 1) * 128], "xt0", [], RWB)
        self.dma("sp", row(3, 0, 16), I["cvec"].rearrange("g (j p) -> (g j) p", p=128), "xt0", [], RWB)
        self.dma("sp", row(3, 16, 2), I["subln_g"], "xt0", [], RWB)
        pi, ps, pb = self.ps_get()

        def ftr(e, ps=ps):
            r = None
            for a in range(4):
                r = e.transpose(out=ps[:, a * 128:(a + 1) * 128], in_=RW[:, a, :], identity=self.ident[:])
            return r
        P.op("pe", ftr, RWB + C, [pb])
        self.copy("dve", self.VCOL[:].rearrange("p a c -> p (a c)"), ps[:], [pb], V)
        self.ps_put(pi)
        for i, k in enumerate(("lam_q1", "lam_k1", "lam_q2", "lam_k2")):
            src = bass.AP(I[k].tensor, 0, [[0, 128], [64, L], [1, 64]])
            self.dma("sp", self.lamv[:, i], src, "vec", [], V)
        lamv, lamt = self.lamv, self.lamt
        self.tt("dve", lamv[:, 0], lamv[:, 0], lamv[:, 1], ALU.mult, V, V)
        self.tt("dve", lamv[:, 2], lamv[:, 2], lamv[:, 3], ALU.mult, V, V)
        P.op("dve", lambda e: e.reduce_sum(out=lamt[:, 0, :], in_=lamv[:, 0], axis=mybir.AxisListType.X), V, V)
        P.op("dve", lambda e: e.reduce_sum(out=lamt[:, 1, :], in_=lamv[:, 2], axis=mybir.AxisListType.X), V, V)
        self.act(lamt[:, 2, :], lamt[:, 0, :], AF.Exp, V, V)
        self.act(lamt[:, 3, :], lamt[:, 1, :], AF.Exp, V, V)
        self.tt("dve", lamt[:, 4, :], lamt[:, 2, :], lamt[:, 3, :], ALU.subtract, V, V)
        for l in range(L):
            lam_init = 0.8 - 0.6 * math.exp(-0.3 * l)
            self.ts("dve", lamt[:, 4, l:l + 1], lamt[:, 4, l:l + 1], lam_init, ALU.add, V, V)
            self.ts("dve", lamt[:, 5, l:l + 1], lamt[:, 4, l:l + 1], -1.0, ALU.mult, V, V)
            self.ts("dve", lamt[:, 6, l:l + 1], self.sublng[:, l:l + 1], 1.0 - lam_init, ALU.mult, V, V)
        self.act(self.scv[:], self.cvT[:], AF.Silu, V, V)

    W_IN_PARTS = (("a", ((512, 1536), (2560, 5632))), ("b", ((0, 512), (1536, 2560), (5632, 8704))))

    def queue_conversions(self, l, names):
        for k in names:
            if k == "w_mod":
                for b in range(16):
                    def fn(e, b=b, l=l):
                        return [e.dma_start(out=self.WM16[l, b * 64:(b + 1) * 64, :],
                                            in_=self.I["w_mod"][l, b * 64:(b + 1) * 64, :])]
                    self.pending.append((("w_mod", l), f"cvm{l}", fn, self.WM16B[l]))
                continue
            r, c = BIGW[k]
            src, dst = self.I[k], self.W16[k]
            if k == "w_in":
                for part, ranges in self.W_IN_PARTS:
                    for b in range(8):
                        for (c0, c1) in ranges:
                            def fn(e, src=src, dst=dst, b=b, c0=c0, c1=c1, l=l):
                                return [e.dma_start(out=dst[l, b * 128:(b + 1) * 128, c0:c1],
                                                    in_=src[l, b * 128:(b + 1) * 128, c0:c1])]
                            self.pending.append((("w_in_" + part, l), f"cv_w_in_{part}{l}", fn,
                                                 self.W16B[("w_in_" + part, l)]))
                continue
            rb = max(16, min(r, (2 << 20) // (c * 4)))
            for b in range((r + rb - 1) // rb):
                r0, r1 = b * rb, min(r, (b + 1) * rb)

                def fn(e, src=src, dst=dst, r0=r0, r1=r1, l=l):
                    return [e.dma_start(out=dst[l, r0:r1, :], in_=src[l, r0:r1, :])]
                self.pending.append(((k, l), f"cv_{k}{l}", fn, self.W16B[(k, l)]))

    def issue_piece(self, deps=()):
        key, sem, fn, buf = self.pending.popleft()
        self.P.dma("pool", fn, sem, [], [buf], after=list(deps))

    def flush_until(self, key):
        while any(p[0] == key for p in self.pending):
            self.issue_piece()

    def pace(self, deps):
        if not self.pending:
            return
        self.pace_i += 1
        k0 = self.pending[0][0]
        rate = (2 if self.pace_i < 40 else 1) if (k0[1] == 0 or k0[0] == "w_mod") else 5
        if self.pace_i % rate == 0:
            self.issue_piece(deps)

    W_IN_PARTS = (("a", ((512, 1536), (2560, 5632))), ("b", ((0, 512), (1536, 2560), (5632, 8704))))

    def compute_mod(self, l):
        P, I = self.P, self.I
        V = [self.VECB]
        self.flush_until(("w_mod", l))
        for cb in range(12):
            slot = self.wr_i; self.wr_i = (slot + 1) % self.NWR
            wt = self.WR[:, slot]
            self.dma("sp", wt, self.WM16[l, :, cb * T:(cb + 1) * T].rearrange("(k p) c -> p k c", p=128),
                     f"wr{slot}", [self.WM16B[l]], [self.WRB[slot]])
            src = bass.AP(I["b_mod"].tensor, l * 6 * D + cb * T, [[0, 2], [1, T]])
            self.dma("sp", self.bmod2[:], src, "bmod", [], [self.BMODB])
            pi, ps, pb = self.ps_get()
            self.mm(ps[0:2, :], [(self.scv[:, kc, :], wt[:, kc, :]) for kc in range(8)], [self.WRB[slot]] + V, [pb])
            ta, tb = self.tmp()
            self.tt("dve", ta[0:2, :], ps[0:2, :], self.bmod2[:], ALU.add, [pb, self.BMODB], [tb])
            self.ps_put(pi)
            p2i, ps2, pb2 = self.ps_get()

            def ftr(e, ta=ta, ps2=ps2):
                r = None
                for q in range(4):
                    r = e.transpose(out=ps2[:, 2 * q:2 * q + 2], in_=ta[0:2, q * 128:(q + 1) * 128],
                                    identity=self.ident[0:2, 0:2])
                return r
            self.P.op("pe", ftr, [tb, self.CONSTB], [pb2])
            self.copy("dve", self.modT[:, l, :, 4 * cb:4 * cb + 4], ps2[:, 0:8].rearrange("p (q g) -> p g q", g=2),
                      [pb2], [self.MODTB[l]])
            self.ps_put(p2i)
        cf = self.coef
        rd = [self.MODTB[l]] + V
        wr = [self.COEFB[l]]
        for g in range(2):
            m = self.modT[:, l, g]
            self.stt("dve", cf[:, l, g, 0], m[:, 8:16], 1.0, self.vD["g_pre_mix"][:, l], ALU.add, ALU.mult, rd, wr)
            self.copy("dve", cf[:, l, g, 1], m[:, 0:8], rd, wr)
            self.tt("dve", cf[:, l, g, 2], m[:, 16:24], self.vD["g_post_mix"][:, l], ALU.mult, rd, wr)
            self.stt("dve", cf[:, l, g, 3], m[:, 32:40], 1.0, self.vD["g_pre_mlp"][:, l], ALU.add, ALU.mult, rd, wr)
            self.copy("dve", cf[:, l, g, 4], m[:, 24:32], rd, wr)
            self.tt("dve", cf[:, l, g, 5], m[:, 40:48], self.vD["g_post_mlp"][:, l], ALU.mult, rd, wr)

    def load_x(self, kind, ti, l):
        gi = ti if kind == "p" else NPT + ti
        if l > 0:
            self.dma("sp", self.XA[:], self.XS[gi], "xl", [self.XSB[gi]], self.XAB)
            return
        src = self.I["xp"] if kind == "p" else self.I["xs"]
        for tb in range(4):
            xi = self.xt_i; self.xt_i = 1 - xi
            r0 = ti * T + tb * 128
            self.dma("sp", self.XT[:, xi, :], src[r0:r0 + 128, :], f"xt{xi}", [], [self.XTB[xi]])
            for g in range(2):
                pi, ps, pb = self.ps_get()
                xt = self.XT

                def fn(e, xi=xi, g=g, ps=ps):
                    r = None
                    for q in range(4):
                        kc = g * 4 + q
                        r = e.transpose(out=ps[:, q * 128:(q + 1) * 128], in_=xt[:, xi, kc * 128:(kc + 1) * 128],
                                        identity=self.ident[:])
                    return r
                self.P.op("pe", fn, [self.XTB[xi], self.CONSTB], [pb])
                eng = "act" if g == 0 else "dve"
                self.copy(eng, self.XA[:, g * 4:(g + 1) * 4, tb * 128:(tb + 1) * 128],
                          ps[:].rearrange("p (q t) -> p q t", q=4), [pb], self.XAB[g * 4:(g + 1) * 4])
                self.ps_put(pi)

    def store_x(self, kind, ti, l):
        gi = ti if kind == "p" else NPT + ti
        if l < self.nlayers - 1:
            self.dma("act", self.XS[gi], self.XA[:], "st_xa", self.XAB, [self.XSB[gi]])
            return
        dst = self.O["yp"] if kind == "p" else self.O["ys"]
        XA = self.XA
        for tb in range(4):
            si = self.stg_i; self.stg_i = 1 - si
            for g in range(2):
                pi, ps, pb = self.ps_get()

                def fn(e, g=g, ps=ps, tb=tb):
                    r = None
                    for q in range(4):
                        kc = g * 4 + q
                        r = e.transpose(out=ps[:, q * 128:(q + 1) * 128], in_=XA[:, kc, tb * 128:(tb + 1) * 128],
                                        identity=self.ident[:])
                    return r
                self.P.op("pe", fn, self.XAB[g * 4:(g + 1) * 4] + [self.CONSTB], [pb])
                eng = "act" if g == 0 else "dve"
                self.copy(eng, self.STG[:, si, g * T:(g + 1) * T], ps[:], [pb], [self.STGB[si]])
                self.ps_put(pi)
            r0 = ti * T + tb * 128
            tok = self.dma("act", dst[r0:r0 + 128, :], self.STG[:, si, :], f"stg{si}", [self.STGB[si]], [])
            self.out_toks.append(tok)

    def stats_add(self, ps, pb, src, src_bufs, i, n):
        sq, sqb = self.sq()
        sq = sq[:, :src.shape[-1]]
        self.act(sq, src, AF.Square, src_bufs, [sqb])
        self.mm(ps, [(self.ones_bf[:], sq)], [sqb, self.CONSTB], [pb], start=(i == 0), stop=(i == n - 1))

    def stats_rstd(self, ps, pb, nfeat, nt=T):
        ta, tb = self.tmp()
        self.act(ta[:, :nt], ps[:, :nt], AF.Ln, [pb, self.CONSTB], [tb], bias=self.eps_col[:], scale=1.0 / nfeat)
        self.act(self.RSTD[:, :nt], ta[:, :nt], AF.Exp, [tb], [self.RSTDB], scale=-0.5)

    def norm_mod(self, l, g, ia, ib):
        pi, ps, pb = self.ps_get()
        for kc in range(8):
            if kc % 2 == 0:
                self.stats_add(ps[:], pb, self.XA[:, kc, :], [self.XAB[kc]], kc, 8)
            else:
                sq, sqb = self.sq()
                self.tt("dve", sq, self.XA[:, kc, :], self.XA[:, kc, :], ALU.mult, [self.XAB[kc]], [sqb])
                self.mm(ps[:], [(self.ones_bf[:], sq)], [sqb, self.CONSTB], [pb], start=False, stop=(kc == 7))
        self.stats_rstd(ps, pb, D)
        self.ps_put(pi)
        cf = self.coef
        for kc in range(8):
            ta, tb = self.tmp()
            self.stt("dve", ta, self.XA[:, kc, :], cf[:, l, g, ia, kc:kc + 1], self.RSTD[:],
                     ALU.mult, ALU.mult, [self.XAB[kc], self.RSTDB, self.COEFB[l]], [tb])
            self.act(self.Hh[:, kc, :], ta, AF.Identity, [tb, self.COEFB[l]], [self.HB[kc]],
                     bias=cf[:, l, g, ib, kc:kc + 1])

    def post_norm_residual(self, l, g, ig):
        cf = self.coef
        for j in range(8):
            ta, tb = self.tmp()
            self.stt("dve", ta, self.XB[:, j, :], cf[:, l, g, ig, j:j + 1], self.RSTD[:], ALU.mult, ALU.mult,
                     [self.XBB[j], self.RSTDB, self.COEFB[l]], [tb])
            self.tt("dve", self.XA[:, j, :], self.XA[:, j, :], ta, ALU.add, [tb, self.XAB[j]], [self.XAB[j]])

    def load_slab(self, wname, l, k0, kcn, c0, w):
        slot = self.wr_i; self.wr_i = (slot + 1) % self.NWR
        W = self.W16[wname]
        key = wname
        if wname == "w_in":
            key = "w_in_a" if (512 <= c0 < 1536 or 2560 <= c0 < 5632) else "w_in_b"
        self.flush_until((key, l))
        self.dma("sp", self.WR[:, slot, 0:kcn, 0:w],
                 W[l, k0 * 128:(k0 + kcn) * 128, c0:c0 + w].rearrange("(k p) c -> p k c", p=128),
                 f"wr{slot}", [self.W16B[(key, l)]], [self.WRB[slot]])
        self.pace([self.WRB[slot]])
        return slot

    def proj_fm(self, wname, l, c0, nchunks, rhs, handler):
        kcn = len(rhs)
        j = 0
        while j < nchunks:
            nj = min(4, nchunks - j)
            if kcn <= 8:
                slot = self.load_slab(wname, l, 0, kcn, c0 + j * 128, nj * 128)
                for q in range(nj):
                    pi, ps, pb = self.ps_get()
                    self.mm(ps[:], [(self.WR[:, slot, kc, q * 128:(q + 1) * 128], rhs[kc][0]) for kc in range(kcn)],
                            [self.WRB[slot]] + [r[1] for r in rhs], [pb])
                    handler(j + q, ps, pb)
                    self.ps_put(pi)
            else:
                nks = kcn // 8
                accs = [self.ps_get() for _ in range(nj)]
                for ks in range(nks):
                    slot = self.load_slab(wname, l, ks * 8, 8, c0 + j * 128, nj * 128)
                    for q in range(nj):
                        pi, ps, pb = accs[q]
                        self.mm(ps[:], [(self.WR[:, slot, kc, q * 128:(q + 1) * 128], rhs[ks * 8 + kc][0])
                                        for kc in range(8)],
                                [self.WRB[slot]] + [rhs[ks * 8 + kc][1] for kc in range(8)], [pb],
                                start=(ks == 0), stop=(ks == nks - 1))
                for q in range(nj):
                    pi, ps, pb = accs[q]
                    handler(j + q, ps, pb)
                    self.ps_put(pi)
            j += nj

    def proj_tm(self, l, c0, handler):
        slot = self.load_slab("w_in", l, 0, 8, c0, T)
        for tb in range(4):
            pi, ps, pb = self.ps_get()
            self.mm(ps[:], [(self.Hh[:, kc, tb * 128:(tb + 1) * 128], self.WR[:, slot, kc, :]) for kc in range(8)],
                    [self.WRB[slot]] + self.HB, [pb])
            handler(tb, ps, pb)
            self.ps_put(pi)

    def hchunks(self):
        return [(self.Hh[:, kc, :], self.HB[kc]) for kc in range(8)]

    def rope_to(self, ps, pb, out_ap, out_bufs):
        qb, qbb = self.sq()
        self.copy("act", qb, ps[:], [pb], [qbb])
        p2i, ps2, pb2 = self.ps_get()
        self.mm(ps2[:], [(self.perm_bf[:], qb)], [qbb, self.CONSTB], [pb2])
        t1, t1b = self.tmp()
        t2, t2b = self.tmp()
        self.tt("dve", t1, ps[:], self.ROPE[:, 0, :], ALU.mult, [pb, self.ROPEB], [t1b])
        self.tt("dve", t2, ps2[:], self.ROPE[:, 1, :], ALU.mult, [pb2, self.ROPEB], [t2b])
        self.ps_put(p2i)
        self.tt("pool", out_ap, t1, t2, ALU.add, [t1b, t2b], out_bufs)

    def load_rope(self, ti):
        self.dma("sp", self.ROPE[:, 0, :], self.I["rope_c"][:, ti * T:(ti + 1) * T], "rope", [], [self.ROPEB])
        self.dma("sp", self.ROPE[:, 1, :], self.I["rope_s"][:, ti * T:(ti + 1) * T], "rope", [], [self.ROPEB])

    def front_gated_inputs(self, l, s_out, s_bufs, u_out, u_bufs):
        H_ = self.hchunks()

        def h_cg(j, ps, pb):
            self.copy("act", self.XB[:, j, :], ps[:], [pb], [self.XBB[j]])
        self.proj_fm("w_in", l, C_CG * 128, 4, H_, h_cg)

        def h_xin(j, ps, pb):
            o = s_out(j)
            i0 = ps[:] if len(o.shape) == 2 else ps[:].rearrange("p (s t) -> p s t", s=2)
            i1 = self.XB[:, j, :] if len(o.shape) == 2 else self.XB[:, j, :].rearrange("p (s t) -> p s t", s=2)
            self.tt("dve", o, i0, i1, ALU.mult, [pb, self.XBB[j]], [s_bufs[j]])
        self.proj_fm("w_in", l, C_XIN * 128, 4, H_, h_xin)

        def h_zg(j, ps, pb):
            self.act(self.XB[:, 4 + j, :], ps[:], AF.Sigmoid, [pb], [self.XBB[4 + j]])
        self.proj_fm("w_in", l, C_ZG * 128, 4, H_, h_zg)

        def h_za(j, ps, pb):
            o = u_out(j)
            i0 = ps[:] if len(o.shape) == 2 else ps[:].rearrange("p (s t) -> p s t", s=2)
            i1 = self.XB[:, 4 + j, :] if len(o.shape) == 2 else self.XB[:, 4 + j, :].rearrange("p (s t) -> p s t", s=2)
            self.tt("dve", o, i0, i1, ALU.mult, [pb, self.XBB[4 + j]], [u_bufs[j]])
        self.proj_fm("w_in", l, C_ZA * 128, 4, H_, h_za)

    def front_bg(self, l):
        def h_bg(j, ps, pb):
            self.copy("act", self.BG[:, j, :], ps[:], [pb], [self.BGB[j]])
        self.proj_fm("w_in", l, C_BG * 128, 4, self.hchunks(), h_bg)

    def front_prompt(self, l, ti):
        P = self.P
        P.op("pool", lambda e: e.memset(self.SW[:], 0.0), [], self.SWB)
        P.op("pool", lambda e: e.memset(self.UW[:], 0.0), [], self.UWB)
        sview = lambda j: self.SW[:, j, 0:516].rearrange("p (s t) -> p s t", s=2)[:, :, 1:257]
        uview = lambda j: self.UW[:, j, 0:572].rearrange("p (s t) -> p s t", s=2)[:, :, 15:271]
        self.front_gated_inputs(l, sview, self.SWB, uview, self.UWB)
        if self.stage == 4 and self.sub == 1:
            return
        self.front_bg(l)
        H_ = self.hchunks()

        def h_q(h, ps, pb):
            self.copy("dve", self.QM[:, h, :], ps[:], [pb], [self.QMB[h]])
        self.proj_fm("w_in", l, C_Q * 128, 8, H_, h_q)

        def h_k(h, ps, pb):
            self.copy("act", self.R[:, h * T:(h + 1) * T], ps[:], [pb], [self.RB[h]])
        self.proj_fm("w_in", l, C_K * 128, 8, H_, h_k)
        if self.stage == 4 and self.sub == 2:
            return
        for which, c0, dst in (("v", C_V * 128, self.O["nv"]), ("k", C_K * 128, self.O["nk"])):
            for hf in range(2):
                def h_tm(tb, ps, pb, hf=hf, which=which, dst=dst):
                    ta, tbb = self.tmp()
                    ti_ = self.TMPB.index(tbb)
                    self.copy("act", ta, ps[:], [pb], [tbb])
                    if which == "v":
                        u = 8 + 2 * tb + hf
                        self.copy("dve", self.R[:, u * T:(u + 1) * T], ta, [tbb], [self.RB[u]])
                    b = ti * 2 + tb // 2
                    r0 = (tb % 2) * 128
                    if self.sub != 5:
                        tok = self.dma("act", dst[b, l, r0:r0 + 128, hf * T:(hf + 1) * T], ta, f"st_tmp{ti_}", [tbb], [])
                        self.out_toks.append(tok)
                self.proj_tm(l, c0 + hf * T, h_tm)

    def front_sample_a(self, l, ti):
        self.load_rope(ti)
        self.front_gated_inputs(l, lambda j: self.AIN[:, j, :], self.AINB, lambda j: self.ZS[:, j, :], self.ZSB)
        t0 = ti * T
        self.dma("act", self.SUS[:, :, 1 + t0:1 + t0 + T], self.AIN[:], "st_ain", self.AINB, [self.SUSB])
        self.dma("act", self.SUU[:, :, 15 + t0:15 + t0 + T], self.ZS[:], "st_zs", self.ZSB, [self.SUUB])
        halo = self.GINP[4].rearrange("r (q e) -> (r q) e", e=128)
        if ti == 0:
            self.dma("act", halo[:, 0:60].rearrange("p (c k) -> p c k", k=15), self.ZS[:, :, 0:15], "st_zs",
                     self.ZSB, [self.GINB[4]])
            self.dma("act", halo[:, 120:124].rearrange("p (c k) -> p c k", k=1), self.AIN[:, :, 0:1], "st_ain",
                     self.AINB, [self.GINB[4]], slow=True)
        if ti == NST - 1:
            self.dma("act", halo[:, 60:120].rearrange("p (c k) -> p c k", k=15), self.ZS[:, :, T - 15:T], "st_zs",
                     self.ZSB, [self.GINB[4]])
            self.dma("act", halo[:, 124:128].rearrange("p (c k) -> p c k", k=1), self.AIN[:, :, T - 1:T], "st_ain",
                     self.AINB, [self.GINB[4]], slow=True)
        H_ = self.hchunks()

        def h_k(h, ps, pb):
            self.rope_to(ps, pb, self.QM[:, h, :], [self.QMB[h]])
        self.proj_fm("w_in", l, C_K * 128, 8, H_, h_k)
        for i in range(2):
            self.dma("act", self.GINP[i][:, t0:t0 + T].rearrange("(h p) t -> p h t", p=128),
                     self.QM[:, 4 * i:4 * i + 4, :], "st_qm", self.QMB[4 * i:4 * i + 4], [self.GINB[i]])
        for hf in range(2):
            def h_v(tb, ps, pb, hf=hf):
                self.copy("act" if tb % 2 else "dve", self.OT[:, tb * 2 + hf, :], ps[:], [pb], [self.OTB[tb * 2 + hf]])
            self.proj_tm(l, C_V * 128 + hf * T, h_v)
        for tb in range(4):
            kcg = ti * 4 + tb
            for i in range(2):
                self.dma("act", self.GINP[2 + i][:, kcg * 128:(kcg + 1) * 128].rearrange("(h p) v -> p h v", p=128),
                         self.OT[:, tb * 2 + i, :].rearrange("p (b v) -> p b v", v=128), "st_ot",
                         [self.OTB[tb * 2 + i]], [self.GINB[2 + i]])

    def front_sample_b(self, l, ti):
        self.load_rope(ti)
        t0 = ti * T
        self.dma("sp", self.SW[:, :, 0:514], self.SUS[:, :, t0:t0 + 514], "sw", [self.SUSB], self.SWB)
        self.dma("sp", self.UW[:, :, 0:542], self.SUU[:, :, t0:t0 + 542], "uw", [self.SUUB], self.UWB)
        self.front_bg(l)

        def h_q(h, ps, pb):
            self.rope_to(ps, pb, self.QM[:, h, :], [self.QMB[h]])
        self.proj_fm("w_in", l, C_Q * 128, 8, self.hchunks(), h_q)

    def build_dg3(self, l):
        for c in range(4):
            self.tt("pool", self.DG3[:, c], self.ident_bf[:].unsqueeze(1).broadcast_to([128, 3, 128]),
                    self.wsc[:, l, c, :].unsqueeze(2).broadcast_to([128, 3, 128]), ALU.mult,
                    [self.CONSTB, self.VECB], [self.DG3B])

    def conv_a(self, l, segs):
        for c in range(4):
            pi, ps, pb = self.ps_get()
            for (so, do, n) in segs:
                self.mm(ps[:, do:do + n], [(self.DG3[:, c, j, :], self.SW[:, c, so + j:so + j + n]) for j in range(3)],
                        [self.DG3B, self.SWB[c]], [pb])
            self.tt("dve", self.AIN[:, c, :], ps[:], self.BG[:, c, :], ALU.mult, [pb, self.BGB[c]], [self.AINB[c]])
            self.ps_put(pi)

    def conv_c(self, l, segs):
        p1i, ps1, pb1 = self.ps_get()
        p2i, ps2, pb2 = self.ps_get()
        for c in range(4):
            d = 0
            self.tt("dve", self.DG31[:, d], self.ident_bf[:].unsqueeze(1).broadcast_to([128, 31, 128]),
                    self.wcf_l[l][:, c, :].unsqueeze(2).broadcast_to([128, 31, 128]), ALU.mult,
                    [self.CONSTB, self.VECB], [self.DG31B[d]])
            pi, ps, pb = self.ps_get()
            for (so, do, n) in segs:
                self.mm(ps[:, do:do + n],
                        [(self.DG31[:, d, j, :], self.UW[:, c, so + j:so + j + n]) for j in range(31)],
                        [self.DG31B[d], self.UWB[c]], [pb])
            bcol = self.v5["cf_conv_b"][:, l, c:c + 1]
            self.act(self.XB[:, c, :], ps[:], AF.Identity, [pb, self.VECB], [self.XBB[c]], bias=bcol)
            zb, zbb = self.sq()
            self.act(zb, ps[:], AF.Identity, [pb, self.VECB], [zbb], bias=bcol)
            self.mm(ps1[:], [(self.ones_bf[:], zb)], [zbb, self.CONSTB], [pb1], start=(c == 0), stop=(c == 3))
            self.ps_put(pi)
            self.stats_add(ps2[:], pb2, self.XB[:, c, :], [self.XBB[c]], c, 4)
        mean, meanb = self.tmp()
        msq, msqb = self.tmp()
        var, varb = self.tmp()
        self.ts("dve", mean, ps1[:], 1.0 / 512, ALU.mult, [pb1], [meanb])
        self.tt("dve", msq, mean, mean, ALU.mult, [meanb], [msqb])
        self.stt("dve", var, ps2[:], 1.0 / 512, msq, ALU.mult, ALU.subtract, [pb2, msqb], [varb])
        self.ps_put(p1i); self.ps_put(p2i)
        sd, sdb = self.tmp()
        self.act(sd, var, AF.Ln, [varb, self.CONSTB], [sdb], bias=self.eps_col[:], scale=1.0)
        self.act(self.RSTD[:], sd, AF.Exp, [sdb], [self.RSTDB], scale=-0.5)
        for c in range(4):
            t1, t1b = self.tmp()
            self.tt("dve", t1, self.XB[:, c, :], mean, ALU.subtract, [self.XBB[c], meanb], [t1b])
            self.tt("dve", t1, t1, self.RSTD[:], ALU.mult, [t1b, self.RSTDB], [t1b])
            self.act(self.ZS[:, c, :], t1, AF.Silu, [t1b, self.VECB], [self.ZSB[c]],
                     bias=self.v5["cf_ln_b"][:, l, c:c + 1], scale=self.v5["cf_ln_g"][:, l, c:c + 1])

    def attention_main(self, l, h, q0, nq, chunks, o_out, o_bufs):
        accs = [self.ps_get() for _ in range(3)]
        n = len(chunks)
        sbanks = {}
        DEPTH = 2
        zs = self.z_i; self.z_i = 1 - zs
        ZB = self.ZBS[zs]

        def qk(j):
            kT, kb, _, _ = chunks[j]
            a = self.ps_get(); b = self.ps_get()
            sbanks[j] = (a, b)
            self.mm(a[1][:, :nq], [(kT[0:64, :], self.QM[0:64, h, q0:q0 + nq])], kb + [self.QMB[h]], [a[2]],
                    tile_pos=(0, 0))
            self.mm(b[1][:, :nq], [(kT[64:128, :], self.QM[64:128, h, q0:q0 + nq])], kb + [self.QMB[h]], [b[2]],
                    tile_pos=(64, 0))

        def pv(j):
            _, _, v, vb = chunks[j]
            a, b = sbanks.pop(j)
            e0 = 2 * (j % 3)
            for m, s in enumerate((a, b)):
                self.act(self.E[:, e0 + m, :nq], s[1][:, :nq], AF.Exp, [s[2]], [self.EB[e0 + m]], scale=0.125)
                self.ps_put(s[0])
            zp = self.ZACC[:, zs, 0, :nq]
            if j == 0:
                self.copy("dve", zp, self.E[:, e0, :nq], [self.EB[e0]], [ZB])
            else:
                self.tt("dve", zp, zp, self.E[:, e0, :nq], ALU.add, [self.EB[e0], ZB], [ZB])
            for m in range(2):
                self.mm(accs[m][1][:, :nq], [(v, self.E[:, e0 + m, :nq])], vb + [self.EB[e0 + m]], [accs[m][2]],
                        start=(j == 0), stop=(j == n - 1))
            self.mm(accs[2][1][:, :nq], [(self.ones_bf[:], self.E[:, e0 + 1, :nq])], [self.CONSTB, self.EB[e0 + 1]],
                    [accs[2][2]], start=(j == 0), stop=(j == n - 1))
        for j in range(min(DEPTH, n)):
            qk(j)
        for j in range(n):
            pv(j)
            if j + DEPTH < n:
                qk(j + DEPTH)
        o0, o0b = self.tmp(); o1, o1b = self.tmp()
        self.copy("dve", o0[:, :nq], accs[0][1][:, :nq], [accs[0][2]], [o0b])
        self.copy("dve", o1[:, :nq], accs[1][1][:, :nq], [accs[1][2]], [o1b])
        self.copy("dve", self.ZACC[:, zs, 1, :nq], accs[2][1][:, :nq], [accs[2][2]], [ZB])
        for a in accs:
            self.ps_put(a[0])
        return dict(l=l, nq=nq, zs=zs, o0=o0, o0b=o0b, o1=o1, o1b=o1b, o_out=o_out, o_bufs=o_bufs)

    def attention_fin(self, st):
        if st is None:
            return
        l, nq, zs = st["l"], st["nq"], st["zs"]
        o0, o0b, o1, o1b = st["o0"], st["o0b"], st["o1"], st["o1b"]
        ZB = self.ZBS[zs]
        V = [self.VECB]
        r0, r0b = self.tmp(); r1, r1b = self.tmp()
        zi, zps, zpb = self.ps_get()
        self.mm(zps[:, :nq], [(self.ones_f[:], self.ZACC[:, zs, 0, :nq])], [ZB, self.CONSTB], [zpb])
        self.act(r0[:, :nq], zps[:, :nq], AF.Ln, [zpb], [r0b])
        self.ps_put(zi)
        self.act(r0[:, :nq], r0[:, :nq], AF.Exp, [r0b], [r0b], scale=-1.0)
        self.act(r1[:, :nq], self.ZACC[:, zs, 1, :nq], AF.Ln, [ZB], [r1b])
        self.act(r1[:, :nq], r1[:, :nq], AF.Exp, [r1b], [r1b], scale=-1.0)
        self.tt("dve", o0[:, :nq], o0[:, :nq], r0[:, :nq], ALU.mult, [o0b, r0b], [o0b])
        self.stt("dve", o1[:, :nq], o1[:, :nq], self.lamt[:, 5, l:l + 1], r1[:, :nq], ALU.mult, ALU.mult,
                 [o1b, r1b] + V, [o1b])
        self.tt("dve", o0[:, :nq], o0[:, :nq], o1[:, :nq], ALU.add, [o0b, o1b], [o0b])
        pi, ps, pb = self.ps_get()
        self.stats_add(ps[:, :nq], pb, o0[:, :nq], [o0b], 0, 1)
        self.stats_rstd(ps, pb, 128, nq)
        self.ps_put(pi)
        self.tt("dve", o0[:, :nq], o0[:, :nq], self.RSTD[:, :nq], ALU.mult, [o0b, self.RSTDB], [o0b])
        self.act(st["o_out"], o0[:, :nq], AF.Identity, [o0b] + V, st["o_bufs"], scale=self.lamt[:, 6, l:l + 1])

    def attn_prompt(self, l):
        prev = None
        for bi in range(2):
            for h in range(H):
                chunks = []
                for tb in (2 * bi, 2 * bi + 1):
                    kT = self.R[:, h * T + tb * 128:h * T + (tb + 1) * 128]
                    u0 = 8 + 2 * tb + (h // 4)
                    v = self.R[:, u0 * T + (h % 4) * 128:u0 * T + (h % 4 + 1) * 128]
                    chunks.append((kT, [self.RB[h]], v, [self.RB[u0]]))
                st = self.attention_main(l, h, bi * 256, 256, chunks, self.OT[:, h, bi * 256:(bi + 1) * 256],
                                         [self.OTB[h]])
                self.attention_fin(prev)
                prev = st
        self.attention_fin(prev)

    def load_kv(self, h):
        s = h % 2
        ku = self.RB[9 * s:9 * s + 9]
        vu = self.RB[18 + 9 * s:18 + 9 * s + 9]
        K = self.R[:, s * 4608:(s + 1) * 4608]
        Vv = self.R[:, 9216 + s * 4608:9216 + (s + 1) * 4608].rearrange("p (k v) -> p k v", v=128)
        GK = self.GOUTP[h // 4]
        GV = self.GOUTP[2 + h // 4]
        r0 = (h % 4) * 128

        def fk(e):
            return [e.dma_start(out=K[:, 0:512], in_=self.KCTX[:, h, :]),
                    e.dma_start(out=K[:, 512:2560], in_=GK[r0:r0 + 128, :]),
                    e.dma_start(out=K[:, 2560:4608], in_=GK[512 + r0:512 + r0 + 128, :])]
        self.P.dma("sp", fk, f"kv{s}", [self.KCTXB, self.GOUTB[h // 4]], ku, n=3)

        def fv(e):
            return [e.dma_start(out=Vv[:, 0:4, :], in_=self.VCTX[h]),
                    e.dma_start(out=Vv[:, 4:20, :],
                                in_=GV[r0:r0 + 128, :].rearrange("p (k v) -> p k v", v=128)),
                    e.dma_start(out=Vv[:, 20:36, :],
                                in_=GV[512 + r0:512 + r0 + 128, :].rearrange("p (k v) -> p k v", v=128))]
        self.P.dma("sp", fv, f"kv{2 + s}", [self.VCTXB, self.GOUTB[2 + h // 4]], vu, n=3)
        if self.pending:
            self.issue_piece([vu[0]])

    def attn_sample(self, l):
        self.load_kv(0)
        prev = None
        for h in range(H):
            if h + 1 < H:
                self.load_kv(h + 1)
            s = h % 2
            chunks = []
            for j in range(36):
                kT = self.R[:, s * 4608 + j * 128:s * 4608 + (j + 1) * 128]
                v = self.R[:, 9216 + s * 4608 + j * 128:9216 + s * 4608 + (j + 1) * 128]
                chunks.append((kT, self.RB[9 * s:9 * s + 9], v, self.RB[18 + 9 * s:18 + 9 * s + 9]))
            st = self.attention_main(l, h, 0, T, chunks, self.OT[:, h, :], [self.OTB[h]])
            self.attention_fin(prev)
            prev = st
        self.attention_fin(prev)

    def merge(self, l):
        H_ = self.hchunks()
        ain = [(self.AIN[:, c, :], self.AINB[c]) for c in range(4)]
        ot = [(self.OT[:, c, :], self.OTB[c]) for c in range(8)]
        zs = [(self.ZS[:, c, :], self.ZSB[c]) for c in range(4)]
        for pas, (cg, wname, rhs) in enumerate(((C_GA, "sc_w_out", ain), (C_GB, "attn_w_out", ot),
                                                (C_GC, "cf_w_out", zs))):
            for jg in range(2):
                sg = [self.tmp() for _ in range(4)]

                def h_gate(q, ps, pb, sg=sg):
                    self.act(sg[q][0], ps[:], AF.Sigmoid, [pb], [sg[q][1]])
                self.proj_fm("w_in", l, cg * 128 + jg * T, 4, H_, h_gate)

                def h_y(q, ps, pb, sg=sg, jg=jg, pas=pas):
                    j = jg * 4 + q
                    if pas == 0:
                        self.tt("dve", self.XB[:, j, :], ps[:], sg[q][0], ALU.mult, [pb, sg[q][1]], [self.XBB[j]])
                    elif pas == 1:
                        t, tb = self.tmp()
                        self.tt("dve", t, ps[:], sg[q][0], ALU.mult, [pb, sg[q][1]], [tb])
                        self.tt("pool", self.XB[:, j, :], self.XB[:, j, :], t, ALU.add, [tb, self.XBB[j]],
                                [self.XBB[j]])
                    else:
                        t, tb = self.tmp()
                        self.stt("dve", t, ps[:], self.vD["cf_b_out"][:, l, j:j + 1], sg[q][0], ALU.add, ALU.mult,
                                 [pb, sg[q][1], self.VECB], [tb])
                        self.tt("pool", self.QM[:, j, :], self.XB[:, j, :], t, ALU.add, [tb, self.XBB[j]],
                                [self.QMB[j]])
                self.proj_fm(wname, l, jg * T, 4, rhs, h_y)

    def out_proj_residual(self, l, g, wname, rhs, ig):
        si, sps, spb = self.ps_get()

        def h(j, ps, pb):
            self.copy("dve", self.XB[:, j, :], ps[:], [pb], [self.XBB[j]])
            self.stats_add(sps[:], spb, ps[:], [pb], j, 8)
        self.proj_fm(wname, l, 0, 8, rhs, h)
        self.stats_rstd(sps, spb, D)
        self.ps_put(si)
        self.post_norm_residual(l, g, ig)

    def mlp(self, l, g):
        self.norm_mod(l, g, 3, 4)

        def h_ff1(j, ps, pb):
            t, tb = self.tmp()
            self.act(t, ps[:], AF.Relu, [pb], [tb])
            self.tt("pool", self.R[:, j * T:(j + 1) * T], t, t, ALU.mult, [tb], [self.RB[j]])
        self.proj_fm("w_ff1", l, 0, 32, self.hchunks(), h_ff1)
        f = [(self.R[:, j * T:(j + 1) * T], self.RB[j]) for j in range(32)]
        self.out_proj_residual(l, g, "w_ff2", f, 5)

    def back_half(self, l, g, kind, ti):
        self.merge(l)
        m = [(self.QM[:, j, :], self.QMB[j]) for j in range(8)]
        self.out_proj_residual(l, g, "w_o", m, 2)
        self.mlp(l, g)
        self.store_x(kind, ti, l)

    def prompt_tile(self, l, ti):
        stage = self.stage
        dbg = (l == 0 and ti == 0)
        self.load_x("p", ti, l)
        self.norm_mod(l, 0, 0, 1)
        if dbg:
            self.dump("coef", self.coef[:], self.COEFB)
            self.dump("H", self.Hh[:], self.HB)
        if stage == 3:
            return
        self.front_prompt(l, ti)
        if dbg:
            self.dump("QM", self.QM[:], self.QMB)
            self.dump("SW", self.SW[:], self.SWB)
            self.dump("UW", self.UW[:], self.UWB)
        if stage == 4:
            return
        self.conv_a(l, [(0, 0, 256), (258, 256, 256)])
        self.conv_c(l, [(0, 0, 256), (286, 256, 256)])
        if dbg:
            self.dump("AIN", self.AIN[:], self.AINB)
            self.dump("ZS", self.ZS[:], self.ZSB)
        if stage == 5:
            return
        self.attn_prompt(l)
        if dbg:
            self.dump("OT", self.OT[:], self.OTB)
        if stage == 6:
            return
        self.merge(l)
        if dbg:
            self.dump("M", self.QM[:], self.QMB)
        m = [(self.QM[:, j, :], self.QMB[j]) for j in range(8)]
        self.out_proj_residual(l, 0, "w_o", m, 2)
        if dbg:
            self.dump("XMID", self.XA[:], self.XAB)
        if stage == 7:
            return
        self.mlp(l, 0)
        self.store_x("p", ti, l)
        self.swap_x()

    def swap_x(self):
        self.XA, self.XB = self.XB, self.XA
        self.XAB, self.XBB = self.XBB, self.XAB

    def sample_a(self, l, ti):
        self.load_x("s", ti, l)
        self.norm_mod(l, 1, 0, 1)
        self.front_sample_a(l, ti)
        self.swap_x()

    def sample_b(self, l, ti):
        self.load_x("s", ti, l)
        self.norm_mod(l, 1, 0, 1)
        self.front_sample_b(l, ti)
        self.conv_a(l, [(0, 0, T)])
        self.conv_c(l, [(0, 0, T)])
        self.attn_sample(l)
        self.back_half(l, 1, "s", ti)
        self.swap_x()

    def convert_ctx(self, l):
        for kb in range(4):
            xi = self.xt_i; self.xt_i = 1 - xi
            self.dma("sp", self.XT[:, xi, :], self.I["ck"][l, kb * 128:(kb + 1) * 128, :], f"xt{xi}", [],
                     [self.XTB[xi]])
            for g in range(2):
                pi, ps, pb = self.ps_get()

                def fn(e, xi=xi, g=g, ps=ps):
                    r = None
                    for q in range(4):
                        h = g * 4 + q
                        r = e.transpose(out=ps[:, q * 128:(q + 1) * 128], in_=self.XT[:, xi, h * 128:(h + 1) * 128],
                                        identity=self.ident[:])
                    return r
                self.P.op("pe", fn, [self.XTB[xi], self.CONSTB], [pb])
                self.copy("act" if g else "dve", self.OT[:, g * 4:(g + 1) * 4, kb * 128:(kb + 1) * 128],
                          ps[:].rearrange("p (q t) -> p q t", q=4), [pb], self.OTB[g * 4:(g + 1) * 4])
                self.ps_put(pi)
        self.dma("act", self.KCTX, self.OT[:], "kctx", self.OTB, [self.KCTXB])

        def fv(e):
            return [e.dma_start(out=self.VCTX[h],
                                in_=self.I["cv"][l, :, h * 128:(h + 1) * 128].rearrange("(k p) v -> p k v", p=128))
                    for h in range(H)]
        self.P.dma("pool", fv, "vctx", [], [self.VCTXB], n=H)

    def exchange(self, l):
        rg = [[2 * i, 2 * i + 1] for i in range(self.ncores // 2)]
        for i in range(5):
            def fn(e, i=i):
                return [e.collective_compute("AllGather", ALU.bypass, replica_groups=rg,
                                             ins=[self.GINP[i]], outs=[self.GOUTP[i]])]
            self.P.dma("pool", fn, f"cc{i}", [self.GINB[i]], [self.GOUTB[i]], inc=1)

    def halo_fix(self, l):
        G = self.GOUTP[4]
        h0 = G[0:8, :].rearrange("r (q e) -> (r q) e", e=128)
        h1 = G[8:16, :].rearrange("r (q e) -> (r q) e", e=128)
        hal = self.hal
        self.P.dma("sp", lambda e: [e.dma_start(out=hal[:, 0, :], in_=h0), e.dma_start(out=hal[:, 1, :], in_=h1)],
                   "hal", [self.GOUTB[4]], [self.HALB], n=2)
        self.ts("dve", hal[:, 0, :], hal[:, 0, :], self.masks[:, 0:1], ALU.mult, [self.HALB, self.CONSTB], [self.HALB])
        self.ts("dve", hal[:, 1, :], hal[:, 1, :], self.masks[:, 1:2], ALU.mult, [self.HALB, self.CONSTB], [self.HALB])

        def fw(e):
            return [
                e.dma_start(out=self.SUU[:, :, 0:15], in_=hal[:, 0, 60:120].rearrange("p (c k) -> p c k", k=15), allow_slow_non_contiguous=True),
                e.dma_start(out=self.SUS[:, :, 0:1], in_=hal[:, 0, 124:128].rearrange("p (c k) -> p c k", k=1), allow_slow_non_contiguous=True),
                e.dma_start(out=self.SUU[:, :, 2063:2078], in_=hal[:, 1, 0:60].rearrange("p (c k) -> p c k", k=15), allow_slow_non_contiguous=True),
                e.dma_start(out=self.SUS[:, :, 2049:2050], in_=hal[:, 1, 120:124].rearrange("p (c k) -> p c k", k=1), allow_slow_non_contiguous=True),
            ]
        self.P.dma("act", fw, "halw", [self.HALB], [self.SUUB, self.SUSB], n=4)


    def build_body(self):
        stage = self.stage
        self.setup()
        if stage == 0:
            return
        rest = ["sc_w_out", "attn_w_out", "cf_w_out", "w_o", "w_ff1", "w_ff2"]
        self.queue_conversions(0, ["w_mod", "w_in"] + rest)
        two = self.nlayers > 1
        if two:
            self.queue_conversions(1, ["w_mod", "w_in"] + rest)
        self.flush_until(("w_in_a", 0))
        self.compute_mod(0)
        if stage <= 2:
            return
        for l in range(self.nlayers):
            self.build_dg3(l)
            if self.do_sample:
                self.convert_ctx(l)
                for ti in range(NST):
                    self.sample_a(l, ti)
                self.exchange(l)
            if self.do_prompt:
                for ti in range(NPT if stage >= 99 else 1):
                    self.prompt_tile(l, ti)
                    if ti == 0:
                        if l == 0 and two:
                            self.compute_mod(1)
                        if self.do_sample:
                            self.halo_fix(l)
            else:
                if l == 0 and two:
                    self.compute_mod(1)
                if self.do_sample:
                    self.halo_fix(l)
            if self.do_sample:
                for ti in range(NST):
                    self.sample_b(l, ti)
        while self.pending:
            self.issue_piece()

    def build(self):
        with ExitStack() as st:
            self.st = st
            self.P = Prog(self.nc, st)
            self.P.trace_ops = self.trace_ops
            self.alloc()
            for i in range(self.NTMP):
                self.P.newsem(f"st_tmp{i}")
            self.build_body()
            if self.DBG:
                self.out_toks.append(("dbg", self.P.cnt["dbg"]))
            self.P.wait_all("sp", self.out_toks + [(k, v) for k, v in self.P.cnt.items() if v > 0 and k != 'E_sp'])
            self.P.run_block()
        return self.nc


def _rope_tables(rank):
    t = np.arange(2048, dtype=np.int64) + rank * 2048
    row = (t // 64).astype(np.float32)
    col = (t % 64).astype(np.float32)
    inv = (np.float32(10000.0) ** (-np.arange(16, dtype=np.float32) / np.float32(16))).astype(np.float32)
    C = np.zeros((128, 2048), np.float32)
    S = np.zeros((128, 2048), np.float32)
    for p in range(128):
        d = p % 64
        pos = row if d < 32 else col
        dd = d % 32
        ang = (pos * inv[dd % 16]).astype(np.float32)
        C[p] = np.cos(ang)
        S[p] = -np.sin(ang) if dd < 16 else np.sin(ang)
    return C, S


def _perm():
    Pm = np.zeros((128, 128), np.float32)
    for m in range(128):
        dd = (m % 64) % 32
        partner = m + 16 if dd < 16 else m - 16
        Pm[partner, m] = 1.0
    return Pm


_CACHE = {}


def make_in_maps(inputs):
    f = lambda a: np.ascontiguousarray(np.asarray(a, dtype=np.float32))
    shared = {k: f(inputs[k]) for k in IN_SHAPES if k in inputs}
    ident = np.eye(128, dtype=np.float32)
    perm = _perm()
    xp = f(inputs["x_prompt"]); xs = f(inputs["x_sample"])
    ck = f(inputs["cache_k"]); cv = f(inputs["cache_v"])
    c = f(inputs["c"]); c_ctx = f(inputs["c_ctx"])
    maps = []
    for core in range(8):
        b, r = core // 2, core % 2
        C, S = _rope_tables(r)
        m = dict(shared)
        m["xp"] = np.ascontiguousarray(xp[4 * core:4 * core + 4].reshape(1024, D))
        m["xs"] = np.ascontiguousarray(xs[b, r * 2048:(r + 1) * 2048])
        m["ck"] = np.ascontiguousarray(ck[b].reshape(L, 512, D))
        m["cv"] = np.ascontiguousarray(cv[b].reshape(L, 512, D))
        m["cvec"] = np.ascontiguousarray(np.stack([c_ctx, c[b]], 0))
        m["rope_c"] = C; m["rope_s"] = S; m["perm"] = perm; m["ident"] = ident
        mk = np.zeros((128, 2), np.float32)
        mk[:, 0] = 1.0 if r == 1 else 0.0
        mk[:, 1] = 1.0 if r == 0 else 0.0
        m["masks"] = mk
        maps.append(m)
    return maps


def kernel(**inputs):
    if "nc" not in _CACHE:
        _CACHE["nc"] = Builder().build()
    nc = _CACHE["nc"]
    maps = make_in_maps(inputs)
    res = run_bass_kernel_spmd(nc, maps, core_ids=list(range(8)))
    R = res.results
    y_p = np.concatenate([R[c]["yp"].reshape(4, 256, D) for c in range(8)], 0)
    y_s = np.stack([np.concatenate([R[2 * b]["ys"], R[2 * b + 1]["ys"]], 0) for b in range(4)], 0)
    nk = np.concatenate([R[c]["nk"] for c in range(8)], 0).reshape(32, L, 256, H, 2, 64)
    nv = np.concatenate([R[c]["nv"] for c in range(8)], 0).reshape(32, L, 256, H, 128)
    return (y_p.astype(np.float32), y_s.astype(np.float32), nk.astype(np.float32), nv.astype(np.float32))
```
